# Optimizing a Trainium2 kernel written in Bass

```python
import jax, jax.numpy as jnp
from jax import lax
import numpy as np

D_MODEL = 2048
BATCH = 4
SEQ = 4096
DEPTH = 1

GRID_W = 64
CTX_LEN = 256
HEAD_DIM = 128
N_Q_HEADS = 16
N_KV_HEADS = 4
GQA_GROUP = N_Q_HEADS // N_KV_HEADS
ATTN_W = N_Q_HEADS * HEAD_DIM
KV_W = N_KV_HEADS * HEAD_DIM
LRU_W = D_MODEL
LRU_BLOCKS = 16
LRU_BLOCK_DIM = LRU_W // LRU_BLOCKS
LRU_C = 8.0
CONV_W = 4
CONV_LEFT = 2
D_FF = 5632
Q_BLOCK = 128
ROPE_THETA = 10000.0
EPS = 1e-6
N_MOD = 9
FFN_RES = 0.5
OFF_Q = 0
OFF_K = OFF_Q + ATTN_W
OFF_V = OFF_K + KV_W
OFF_LX = OFF_V + KV_W
OFF_LG = OFF_LX + LRU_W
OFF_GA = OFF_LG + LRU_W
OFF_GL = OFF_GA + D_MODEL
IN_W = OFF_GL + D_MODEL

kernel_name = 'hybrid_gqa_rglru_macaron_dit_layer'


def rms_norm(t, g):
    tf = t.astype(jnp.float32)
    y = tf * lax.rsqrt(jnp.mean(tf * tf, axis=-1, keepdims=True) + EPS)
    return (y * g.astype(jnp.float32)).astype(t.dtype)


def modulate(h, shift, scale):
    return h * (1.0 + scale) + shift


def swiglu(h, wg, wu, wd):
    return (jax.nn.silu(h @ wg) * (h @ wu)) @ wd


def axial_rope_tables(n_tok):
    rows = n_tok // GRID_W
    row = jnp.repeat(jnp.arange(rows, dtype=jnp.float32), GRID_W)
    col = jnp.tile(jnp.arange(GRID_W, dtype=jnp.float32), rows)
    axis_dims = HEAD_DIM // 2
    freqs = ROPE_THETA ** (-jnp.arange(0, axis_dims, 2, dtype=jnp.float32) / axis_dims)
    ang = jnp.concatenate([row[:, None] * freqs, col[:, None] * freqs], axis=-1)
    return jnp.cos(ang), jnp.sin(ang)


def apply_rope(t, cos, sin):
    tf = t.astype(jnp.float32).reshape(t.shape[:-1] + (HEAD_DIM // 2, 2))
    t1, t2 = tf[..., 0], tf[..., 1]
    out = jnp.stack([t1 * cos - t2 * sin, t1 * sin + t2 * cos], axis=-1)
    return out.reshape(t.shape).astype(t.dtype)


def to_heads(t, n_heads):
    b, n, _ = t.shape
    return t.reshape(b, n, n_heads, HEAD_DIM).transpose(0, 2, 1, 3)


def group_queries(q):
    b, _, n, _ = q.shape
    return q.reshape(b, N_KV_HEADS, GQA_GROUP, n, HEAD_DIM)


def latent_attention(q, k_lat, v_lat, k_ctx, v_ctx):
    b, _, _, n, _ = q.shape
    k_all = jnp.concatenate([k_ctx, k_lat], axis=2)
    v_all = jnp.concatenate([v_ctx, v_lat], axis=2)
    n_blk = n // Q_BLOCK
    qb = jnp.moveaxis(q.reshape(b, N_KV_HEADS, GQA_GROUP, n_blk, Q_BLOCK, HEAD_DIM), 3, 0)
    scale = HEAD_DIM ** -0.5

    def one_block(q_blk):
        s = jnp.einsum('bkgqd,bksd->bkgqs', q_blk, k_all, preferred_element_type=jnp.float32) * scale
        p = jax.nn.softmax(s, axis=-1)
        return jnp.einsum('bkgqs,bksd->bkgqd', p.astype(v_all.dtype), v_all)

    ob = lax.map(one_block, qb)
    return ob.transpose(1, 0, 4, 2, 3, 5).reshape(b, n, ATTN_W)


def context_attention(q, k, v):
    b, _, _, n, _ = q.shape
    s = jnp.einsum('bkgqd,bksd->bkgqs', q, k, preferred_element_type=jnp.float32) * (HEAD_DIM ** -0.5)
    p = jax.nn.softmax(s, axis=-1)
    o = jnp.einsum('bkgqs,bksd->bkgqd', p.astype(v.dtype), v)
    return o.transpose(0, 3, 1, 2, 4).reshape(b, n, ATTN_W)


def centred_dwconv(t, w, b):
    n = t.shape[1]
    tp = jnp.pad(t, ((0, 0), (CONV_LEFT, CONV_W - 1 - CONV_LEFT), (0, 0)))
    out = b
    for k in range(CONV_W):
        out = out + tp[:, k:k + n] * w[k]
    return out


def block_diag(t, w, b):
    tb = t.reshape(t.shape[:-1] + (LRU_BLOCKS, LRU_BLOCK_DIM))
    return jnp.einsum('btnd,nde->btne', tb, w).reshape(t.shape) + b


def rglru_coeffs(xc, w_a, b_a, w_x, b_x, lam):
    xf = xc.astype(jnp.float32)
    r = jax.nn.sigmoid(block_diag(xf, w_a, b_a).astype(jnp.float32))
    i = jax.nn.sigmoid(block_diag(xf, w_x, b_x).astype(jnp.float32))
    log_a = -LRU_C * r * jax.nn.softplus(-lam.astype(jnp.float32))
    a = jnp.exp(log_a)
    u = jnp.sqrt(-jnp.expm1(2.0 * log_a)) * (i * xf)
    return a, u


def linear_scan(a, u, h0, reverse):
    def combine(e1, e2):
        a1, b1 = e1
        a2, b2 = e2
        return a1 * a2, a2 * b1 + b2
    a_cum, b_cum = lax.associative_scan(combine, (a, u), axis=1, reverse=reverse)
    return a_cum * h0[:, None, :] + b_cum


def rglru_bidir(xc, h0_f, h0_b, wa, ba, wx, bx, lam):
    a_f, u_f = rglru_coeffs(xc, wa[0], ba[0], wx[0], bx[0], lam[0])
    a_b, u_b = rglru_coeffs(xc, wa[1], ba[1], wx[1], bx[1], lam[1])
    return linear_scan(a_f, u_f, h0_f, False), linear_scan(a_b, u_b, h0_b, True)


def gated_lru_out(h_f, h_b, gate, dtype):
    return ((h_f + h_b) * jax.nn.gelu(gate.astype(jnp.float32))).astype(dtype)


def merge_branches(attn, lru, ga, gl, w_out):
    return (jax.nn.sigmoid(ga) * attn + jax.nn.sigmoid(gl) * lru) @ w_out


def hybrid_layer(x, ctx, c, c_ctx, w_mod, b_mod, norm_g, ffn_wg, ffn_wu, ffn_wd, w_in, w_out,
                 q_norm_g, k_norm_g, conv_w, conv_b, lru_wa, lru_ba, lru_wx, lru_bx, lru_lambda,
                 cos, sin, update_ctx):
    b = x.shape[0]
    mod_x = (jax.nn.silu(c) @ w_mod + b_mod).reshape(b, N_MOD, 1, D_MODEL)
    mod_c = (jax.nn.silu(c_ctx) @ w_mod + b_mod).reshape(N_MOD, D_MODEL)
    sh1, sc1, g1, sh2, sc2, g2, sh3, sc3, g3 = [mod_x[:, i] for i in range(N_MOD)]
    csh1, csc1, cg1, csh2, csc2, cg2, csh3, csc3, cg3 = [mod_c[i] for i in range(N_MOD)]

    x = x + FFN_RES * g1 * swiglu(modulate(rms_norm(x, norm_g[0]), sh1, sc1), ffn_wg[0], ffn_wu[0], ffn_wd[0])
    ctx = ctx + FFN_RES * cg1 * swiglu(modulate(rms_norm(ctx, norm_g[0]), csh1, csc1), ffn_wg[0], ffn_wu[0], ffn_wd[0])

    hx = modulate(rms_norm(x, norm_g[1]), sh2, sc2)
    hc = modulate(rms_norm(ctx, norm_g[1]), csh2, csc2)

    pc = hc @ w_in[:, OFF_K:OFF_LG]
    k_c = rms_norm(to_heads(pc[..., :KV_W], N_KV_HEADS), k_norm_g)
    v_c = to_heads(pc[..., KV_W:2 * KV_W], N_KV_HEADS)
    xc_c = centred_dwconv(pc[..., 2 * KV_W:], conv_w, conv_b)
    zeros = jnp.zeros((b, LRU_W), jnp.float32)
    hf_c, hb_c = rglru_bidir(xc_c, zeros, zeros, lru_wa, lru_ba, lru_wx, lru_bx, lru_lambda)

    p = hx @ w_in
    q = apply_rope(rms_norm(to_heads(p[..., OFF_Q:OFF_K], N_Q_HEADS), q_norm_g), cos, sin)
    k = apply_rope(rms_norm(to_heads(p[..., OFF_K:OFF_V], N_KV_HEADS), k_norm_g), cos, sin)
    v = to_heads(p[..., OFF_V:OFF_LX], N_KV_HEADS)
    attn = latent_attention(group_queries(q), k, v, k_c, v_c)
    xc = centred_dwconv(p[..., OFF_LX:OFF_LG], conv_w, conv_b)
    hf, hb = rglru_bidir(xc, hf_c[:, -1], hb_c[:, 0], lru_wa, lru_ba, lru_wx, lru_bx, lru_lambda)
    lru = gated_lru_out(hf, hb, p[..., OFF_LG:OFF_GA], x.dtype)
    x = x + g2 * merge_branches(attn, lru, p[..., OFF_GA:OFF_GL], p[..., OFF_GL:], w_out)

    if update_ctx:
        pq_c = hc @ w_in[:, OFF_Q:OFF_K]
        pg_c = hc @ w_in[:, OFF_LG:]
        q_c = rms_norm(to_heads(pq_c, N_Q_HEADS), q_norm_g)
        attn_c = context_attention(group_queries(q_c), k_c, v_c)
        lru_c = gated_lru_out(hf_c, hb_c, pg_c[..., :LRU_W], ctx.dtype)
        ctx = ctx + cg2 * merge_branches(attn_c, lru_c, pg_c[..., LRU_W:LRU_W + D_MODEL],
                                         pg_c[..., LRU_W + D_MODEL:], w_out)

    x = x + FFN_RES * g3 * swiglu(modulate(rms_norm(x, norm_g[2]), sh3, sc3), ffn_wg[1], ffn_wu[1], ffn_wd[1])
    if update_ctx:
        ctx = ctx + FFN_RES * cg3 * swiglu(modulate(rms_norm(ctx, norm_g[2]), csh3, csc3), ffn_wg[1], ffn_wu[1], ffn_wd[1])
    return x, ctx


def setup_inputs(seed: int = 0) -> dict:
    key = jax.random.key(seed)
    ks = jax.random.split(key, 24)
    f32 = jnp.float32

    def nrm(k, shape, scale):
        return jax.random.normal(k, shape, f32) * scale

    u = jax.random.uniform(ks[20], (DEPTH, 2, LRU_W), f32, 0.9, 0.999)
    base = u ** (1.0 / LRU_C)
    return {
        'x': nrm(ks[0], (BATCH, SEQ, D_MODEL), 1.0),
        'c': nrm(ks[1], (BATCH, D_MODEL), 1.0),
        'ctx': nrm(ks[2], (BATCH, CTX_LEN, D_MODEL), 1.0),
        'c_ctx': nrm(ks[3], (D_MODEL,), 1.0),
        'w_mod': nrm(ks[4], (DEPTH, D_MODEL, N_MOD * D_MODEL), 0.5 * D_MODEL ** -0.5),
        'b_mod': nrm(ks[5], (DEPTH, N_MOD * D_MODEL), 0.01),
        'norm_g': 1.0 + nrm(ks[6], (DEPTH, 3, D_MODEL), 0.02),
        'ffn_wg': nrm(ks[7], (DEPTH, 2, D_MODEL, D_FF), D_MODEL ** -0.5),
        'ffn_wu': nrm(ks[8], (DEPTH, 2, D_MODEL, D_FF), D_MODEL ** -0.5),
        'ffn_wd': nrm(ks[9], (DEPTH, 2, D_FF, D_MODEL), D_FF ** -0.5),
        'w_in': nrm(ks[10], (DEPTH, D_MODEL, IN_W), D_MODEL ** -0.5),
        'w_out': nrm(ks[11], (DEPTH, D_MODEL, D_MODEL), D_MODEL ** -0.5),
        'q_norm_g': 1.0 + nrm(ks[12], (DEPTH, HEAD_DIM), 0.02),
        'k_norm_g': 1.0 + nrm(ks[13], (DEPTH, HEAD_DIM), 0.02),
        'conv_w': nrm(ks[14], (DEPTH, CONV_W, LRU_W), CONV_W ** -0.5),
        'conv_b': nrm(ks[15], (DEPTH, LRU_W), 0.01),
        'lru_wa': nrm(ks[16], (DEPTH, 2, LRU_BLOCKS, LRU_BLOCK_DIM, LRU_BLOCK_DIM), LRU_BLOCK_DIM ** -0.5),
        'lru_ba': nrm(ks[17], (DEPTH, 2, LRU_W), 0.01),
        'lru_wx': nrm(ks[18], (DEPTH, 2, LRU_BLOCKS, LRU_BLOCK_DIM, LRU_BLOCK_DIM), LRU_BLOCK_DIM ** -0.5),
        'lru_bx': nrm(ks[19], (DEPTH, 2, LRU_W), 0.01),
        'lru_lambda': jnp.log(base) - jnp.log1p(-base),
        'final_norm_g': 1.0 + nrm(ks[21], (D_MODEL,), 0.02),
    }


def reference(x, c, ctx, c_ctx, w_mod, b_mod, norm_g, ffn_wg, ffn_wu, ffn_wd, w_in, w_out,
              q_norm_g, k_norm_g, conv_w, conv_b, lru_wa, lru_ba, lru_wx, lru_bx, lru_lambda,
              final_norm_g):
    cos, sin = axial_rope_tables(x.shape[1])
    for l in range(DEPTH):
        x, ctx = hybrid_layer(x, ctx, c, c_ctx, w_mod[l], b_mod[l], norm_g[l], ffn_wg[l], ffn_wu[l], ffn_wd[l],
                              w_in[l], w_out[l], q_norm_g[l], k_norm_g[l], conv_w[l], conv_b[l],
                              lru_wa[l], lru_ba[l], lru_wx[l], lru_bx[l], lru_lambda[l],
                              cos, sin, l < DEPTH - 1)
    return rms_norm(x, final_norm_g)
```

```python
import numpy as np
from contextlib import ExitStack
import concourse.bass as bass
import concourse.mybir as mybir
from concourse.bass_utils import run_bass_kernel_spmd

F32 = mybir.dt.float32
BF16 = mybir.dt.bfloat16
ALU = mybir.AluOpType
AF = mybir.ActivationFunctionType

P = 128
D = 2048
KC = 16
DFF = 5632
GF = 2
NG = DFF // (128 * GF)
T = 2048
TT = 512
CT = 256
NQ = 16
NKV = 4
EPS = 1e-6
NMOD = 9
NKT = (CT + 2 * T) // 128
SM_SCALE = 128 ** -0.5

SM_CV = 0
SM_NG = 32
SM_BM = 80
SM_LB = 224
SM_LAM = 288
SM_CW = 320
SM_CB = 400
SM_QG = 416
SM_KG = 417
SM_FA = 418
SM_FB = 419
NSM = 420


class Buf:
    __slots__ = ("name", "w", "r")

    def __init__(self, name):
        self.name = name
        self.w = None
        self.r = []


class Op:
    __slots__ = ("eng", "fn", "deps", "sig", "tick", "dma", "idx", "waits", "inc")


class Sched:
    ENG = ("pe", "act", "dve", "pool", "sp")

    def __init__(self):
        self.ops = {e: [] for e in self.ENG}
        self.dma_count = {}
        self.n = 0

    def add(self, eng, fn, r=(), w=(), dma=None, inc=16, nowaw=False, extra=()):
        op = Op()
        op.eng = eng
        op.fn = fn
        op.dma = dma
        op.sig = False
        op.inc = inc
        op.idx = self.n
        self.n += 1
        op.tick = 0
        if dma is not None:
            c = self.dma_count.get(dma, 0) + inc
            self.dma_count[dma] = c
            op.tick = c
        deps = set()
        for b in r:
            if b.w is not None:
                deps.add(b.w)
        for b in w:
            if b.w is not None and not (nowaw and b.w.dma is not None and b.w.dma == dma):
                deps.add(b.w)
            for o in b.r:
                deps.add(o)
        for b in r:
            b.r.append(op)
        for b in w:
            if nowaw and b.w is not None and b.w.dma == dma:
                pass
            b.w = op
            b.r = []
        for o in extra:
            deps.add(o)
        deps.discard(op)
        op.deps = deps
        self.ops[eng].append(op)
        return op

    def finalize(self):
        for e in self.ENG:
            for op in self.ops[e]:
                need = {}
                for d in op.deps:
                    if d.dma is not None:
                        key = ("dma", d.dma)
                        if key not in need or d.tick > need[key].tick:
                            need[key] = d
                    else:
                        if d.eng == "pe" and op.eng == "pe" and op.dma is None:
                            continue
                        key = d.eng
                        if key not in need or d.idx > need[key].idx:
                            need[key] = d
                op.waits = list(need.values())
                for d in op.waits:
                    if d.dma is None:
                        d.sig = True
        for e in self.ENG:
            c = 0
            for op in self.ops[e]:
                if op.dma is None and op.sig:
                    c += 1
                    op.tick = c

    def emit(self, ename, eng, esem, dsem):
        waited = {}
        for op in self.ops[ename]:
            for d in op.waits:
                if d.dma is not None:
                    key = ("dma", d.dma)
                    sem = dsem[d.dma]
                else:
                    key = d.eng
                    sem = esem[d.eng]
                if waited.get(key, 0) < d.tick:
                    eng.wait_ge(sem, d.tick)
                    waited[key] = d.tick
            ins = op.fn(eng)
            if op.dma is not None:
                ins.then_inc(dsem[op.dma], op.inc)
            elif op.sig:
                ins.then_inc(esem[ename], 1)


def build(n_cores=8, debug=False):
    nc = bass.Bass("TRN2", target_bir_lowering=False)
    S = Sched()
    es = ExitStack()

    def din(name, shape, dt=F32):
        return nc.dram_tensor(name, list(shape), dt, kind="ExternalInput").ap()

    def dscr(name, shape, dt):
        return nc.dram_tensor(name, list(shape), dt).ap()

    x_d = din("x", [T, D])
    ctx_d = din("ctx", [CT, D])
    smalls_d = din("smalls", [P, NSM])
    rows_d = din("rows", [4, D])
    gkrow_d = din("gkrow", [1, 256])
    wmod_d = din("wmod", [144 * P, D])
    wff_d = {}
    for f in range(2):
        wff_d[("g", f)] = din(f"wg{f}", [NG * P, 4096])
        wff_d[("u", f)] = din(f"wu{f}", [NG * P, 4096])
        wff_d[("d", f)] = din(f"wd{f}", [NG * P, 4096])
    win_d = din("win", [44 * P, 4096])
    wout_d = din("wout", [8 * P, 4096])
    lruw_d = din("lruw", [P, 64 * P])
    rope_d = din("rope", [P, 2 * T])
    consts_d = din("consts", [P, 2 * P])
    out_d = nc.dram_tensor("out", [T, D], F32, kind="ExternalOutput").ap()

    wffb = {k: dscr(f"b_{k[0]}{k[1]}", [NG * P, 4096], BF16) for k in wff_d}
    winb = dscr("b_win", [44 * P, 4096], BF16)
    woutb = dscr("b_wout", [8 * P, 4096], BF16)
    lruwb = dscr("b_lruw", [P, 64 * P], BF16)
    x1_d = dscr("x1", [T, D], F32)
    qT_d = dscr("qT", [NQ * P, T], BF16)
    kTo_d = dscr("kTo", [P, NKV * T], BF16)
    kTc_d = dscr("kTc", [P, NKV * CT], BF16)
    vo_d = dscr("vo", [T, 512], BF16)
    vc_d = dscr("vc", [CT, 512], BF16)
    kTa_d = dscr("kTa", [2 * P, NKV * T], BF16)
    va_d = dscr("va", [2 * T, 512], BF16)
    lx_d = dscr("lx", [KC * P, T], F32)
    lxc_d = dscr("lxc", [KC * P, CT], F32)
    lg_d = dscr("lg", [KC * P, T], BF16)
    ga_d = dscr("ga", [KC * P, T], BF16)
    gl_d = dscr("gl", [KC * P, T], BF16)
    hf_d = dscr("hf", [KC * P, T], F32)
    mg_d = dscr("mg", [KC * P, T], BF16)
    modraw_d = dscr("modraw", [2, NMOD * D], F32)
    edg_d = dscr("edg", [P, 32], F32)
    edga_d = dscr("edga", [2 * P, 32], F32)
    est_d = dscr("est", [P, 16], F32)
    esta_d = dscr("esta", [2 * P, 16], F32)

    dbg_out = {}
    if debug:
        for nm, src, shp, dt in (("d_x1", x1_d, [T, D], F32), ("d_qT", qT_d, [NQ * P, T], BF16),
                                 ("d_kTa", kTa_d, [2 * P, NKV * T], BF16), ("d_va", va_d, [2 * T, 512], BF16),
                                 ("d_kTc", kTc_d, [P, NKV * CT], BF16),
                                 ("d_lx", lx_d, [KC * P, T], F32), ("d_lg", lg_d, [KC * P, T], BF16),
                                 ("d_hf", hf_d, [KC * P, T], F32), ("d_ga", ga_d, [KC * P, T], BF16), ("d_gl", gl_d, [KC * P, T], BF16), ("d_mg", mg_d, [KC * P, T], BF16),
                                 ("d_modraw", modraw_d, [2, NMOD * D], F32), ("d_esta", esta_d, [2 * P, 16], F32)):
            dbg_out[nm] = (nc.dram_tensor(nm, shp, dt, kind="ExternalOutput").ap(), src)

    class NS_:
        pass
    B = NS_()
    for nm in ("cast_ff0", "cast_ff1", "cast_win", "cast_wout", "cast_lruw", "x1", "qT", "kTo", "kTc", "vo", "vc",
               "kTa", "va", "lx", "lxc", "lg", "ga", "gl", "hf", "mg", "modraw", "edg", "edga", "est", "esta", "out"):
        setattr(B, nm, Buf(nm))

    def sb(name, shape, dt):
        return es.enter_context(nc.sbuf_tensor("s_" + name, list(shape), dt))

    BT = [sb(f"bt{i}", [P, 2056], F32) for i in range(8)]
    bBT = [Buf(f"bt{i}") for i in range(8)]
    HT = sb("ht", [P, 8192], BF16)
    bHT = [Buf(f"ht{i}") for i in range(4)]
    NRING = 5
    WR = [sb(f"wr{i}", [P, 4352], BF16) for i in range(NRING)]
    bWR = [Buf(f"wr{i}") for i in range(NRING)]
    XN = [sb(f"xn{i}", [P, D], F32) for i in range(2)]
    bXN = [Buf(f"xn{i}") for i in range(2)]
    junk = sb("junk", [P, D], BF16); bjunk = Buf("junk")
    GT = [sb(f"gt{i}", [P, D], F32) for i in range(2)]
    bGT = [Buf(f"gt{i}") for i in range(2)]
    smalls = sb("smalls", [P, NSM], F32); bsm = Buf("smalls")
    constsf = sb("constsf", [P, 2 * P], F32); bcf = Buf("constsf")
    identb = sb("identb", [P, P], BF16)
    permb = sb("permb", [P, P], BF16)
    onesb = sb("onesb", [P, P], BF16)
    onesf = sb("onesf", [P, P], F32)
    bconst = Buf("constb")
    epsc = sb("epsc", [P, 2], F32)
    scT = sb("scT", [P, 32], F32); bscT = Buf("scT")
    modT = sb("modT", [P, 144 * 2], F32); bmodT = Buf("modT")
    AB = sb("AB", [P, 2 * 3 * 2 * 16], F32); bAB = Buf("AB")
    stg = [sb(f"stg{i}", [P, 512], F32) for i in range(2)]; bstg = [Buf(f"stg{i}") for i in range(2)]
    stgb = [sb(f"stgb{i}", [P, 512], BF16) for i in range(2)]; bstgb = [Buf(f"stgb{i}") for i in range(2)]
    hid = [sb(f"hid{i}", [P, GF * 512], BF16) for i in range(2)]; bhid = [Buf(f"hid{i}") for i in range(2)]
    sil = [sb(f"sil{i}", [P, 512], F32) for i in range(2)]; bsil = [Buf(f"sil{i}") for i in range(2)]
    ssq = sb("ssq", [P, 8], F32); bssq = Buf("ssq")
    rstd = sb("rstd", [P, 8], F32); brstd = Buf("rstd")
    ropet = sb("ropet", [P, 2 * TT], F32); bropet = Buf("ropet")
    sqb = sb("sqb", [P, TT], BF16); bsqb = Buf("sqb")
    rsd = sb("rsd", [P, TT], F32); brsd = Buf("rsd")
    qnb = sb("qnb", [P, TT], BF16); bqnb = Buf("qnb")
    t1, bt1 = sil[0], bsil[0]
    t2, bt2 = sil[1], bsil[1]
    vst = [sb(f"vst{i}", [P, 256], BF16) for i in range(2)]; bvst = [Buf(f"vst{i}") for i in range(2)]
    qTt = [sb(f"qTt{i}", [P, TT], BF16) for i in range(2)]; bqTt = [Buf(f"qTt{i}") for i in range(2)]
    pT = [sb(f"pT{i}", [P, TT], BF16) for i in range(3)]; bpT = [Buf(f"pT{i}") for i in range(3)]
    rec, brec = rsd, brsd
    nbias = sb("nbias", [P, 1], F32); bnbias = Buf("nbias")
    gkrow, bgkrow = stg[0], bstg[0]
    gkm = sb("gkm", [1, 4], F32); bgkm = Buf("gkm")
    lruWt = [sb(f"lruW{i}", [P, 2 * P], BF16) for i in range(2)]; blruWt = [Buf(f"lruW{i}") for i in range(2)]
    clt = sb("clt", [P, 96], F32); bclt = Buf("clt")
    lxc = sb("lxc", [P, CT + 4], F32); blxc = Buf("lxct")
    cxt = [sb(f"cxt{i}", [P, CT], F32) for i in range(5)]; bcxt = [Buf(f"cxt{i}") for i in range(5)]
    cx16 = sb("cx16", [P, CT], BF16); bcx16 = Buf("cx16")
    hce = sb("hce", [P, 16], F32); bhce = Buf("hce")
    estt = sb("estt", [P, 16], F32); bestt = Buf("estt")
    edgt = sb("edgt", [P, 32], F32); bedgt = Buf("edgt")
    eall = [sb(f"eall{i}", [P, 32], F32) for i in range(2)]; beall = Buf("eall")
    epart = sb("epart", [P, 32], F32); bepart = Buf("epart")
    sall = [sb(f"sall{i}", [P, 16], F32) for i in range(2)]; bsall = Buf("sall")
    spart = sb("spart", [P, 16], F32); bspart = Buf("spart")
    tmpw = sb("tmpw", [P, 256], F32); btmpw = Buf("tmpw")

    PS = [es.enter_context(nc.psum_tensor(f"ps{i}", [P, 512], F32)) for i in range(8)]
    bPS = [Buf(f"ps{i}") for i in range(8)]

    identf = constsf[:, 0:P]

    def HT3(k, a, b):
        return HT[:, k * 512 + a: k * 512 + b]

    def dma(q, out, in_, r, w, key, nowaw=False, extra=()):
        return S.add(q, lambda e: e.dma_start(out=out, in_=in_), r=r, w=w, dma=key, nowaw=nowaw, extra=extra)

    def act(out, in_, func, r, w, **kw):
        return S.add("act", lambda e: e.activation(out=out, in_=in_, func=func, **kw), r=r, w=w)

    def mm(out, lhsT, rhs, start, stop, r, w):
        return S.add("pe", lambda e: e.matmul(out, lhsT, rhs, start=start, stop=stop), r=r, w=w)

    def tr(out, in_, ident, r, w):
        return S.add("pe", lambda e: e.transpose(out, in_, ident), r=r, w=w)

    def tt(eng, out, in0, in1, op, r, w):
        return S.add(eng, lambda e: e.tensor_tensor(out=out, in0=in0, in1=in1, op=op), r=r, w=w)

    def ts(eng, out, in0, s1, s2, op0, op1, r, w):
        if s2 is None:
            return S.add(eng, lambda e: e.tensor_scalar(out=out, in0=in0, scalar1=s1, scalar2=None, op0=op0), r=r, w=w)
        return S.add(eng, lambda e: e.tensor_scalar(out=out, in0=in0, scalar1=s1, scalar2=s2, op0=op0, op1=op1), r=r, w=w)

    def stt(eng, out, in0, scalar, in1, op0, op1, r, w):
        return S.add(eng, lambda e: e.scalar_tensor_tensor(out=out, in0=in0, scalar=scalar, in1=in1, op0=op0, op1=op1),
                     r=r, w=w)

    def cp(eng, out, in_, r, w):
        return S.add(eng, lambda e: e.tensor_copy(out, in_), r=r, w=w)

    def memset(eng, ap, val, w):
        return S.add(eng, lambda e: e.memset(ap, val), w=w)

    def cast(dst, src, rows, buf, key, piece=256):
        for r0 in range(0, rows, piece):
            r1 = min(rows, r0 + piece)
            dma("pool", dst[r0:r1, :], src[r0:r1, :], [], [buf], key, nowaw=True)

    early_pieces = []
    win_pieces = []
    def cast_list(lst, dst, src, rows, buf, key, piece=256):
        for r0 in range(0, rows, piece):
            r1 = min(rows, r0 + piece)
            lst.append(lambda extra=(), r0=r0, r1=r1: dma("pool", dst[r0:r1, :], src[r0:r1, :], [], [buf], key,
                                                          nowaw=True, extra=extra))
    for j0 in range(0, NG * P, 256):
        for k_ in ("g", "u", "d"):
            cast_list(early_pieces, wffb[(k_, 0)][j0:j0 + 256, :], wff_d[(k_, 0)][j0:j0 + 256, :], 256, B.cast_ff0, "c_ff0")
    cast_list(win_pieces, winb, win_d, 44 * P, B.cast_win, "c_win")
    cast_list(win_pieces, lruwb, lruw_d, P, B.cast_lruw, "c_lruw", piece=32)

    late_pieces = []
    def cast_later(dst, src, rows, buf, key, piece=256):
        for r0 in range(0, rows, piece):
            r1 = min(rows, r0 + piece)
            late_pieces.append(lambda r0=r0, r1=r1: dma("pool", dst[r0:r1, :], src[r0:r1, :], [], [buf], key, nowaw=True))
    cast_later(woutb, wout_d, 8 * P, B.cast_wout, "c_wout")
    for k_ in ("g", "u", "d"):
        cast_later(wffb[(k_, 1)], wff_d[(k_, 1)], NG * P, B.cast_ff1, "c_ff1")

    dma("sp", smalls[:], smalls_d, [], [bsm], "smalls")
    dma("sp", constsf[:], consts_d, [], [bcf], "constsf")
    dma("sp", gkrow[0:1, 0:256], gkrow_d, [], [bgkrow], "stg0")
    cp("dve", identb[:], constsf[:, 0:P], [bcf], [bconst])
    cp("dve", permb[:], constsf[:, P:2 * P], [bcf], [bconst])
    memset("dve", onesb[:], 1.0, [bconst])
    memset("dve", onesf[:], 1.0, [bconst])
    memset("dve", epsc[:, 0:1], EPS, [bconst])
    memset("dve", epsc[:, 1:2], 1.0, [bconst])
    S.add("dve", lambda e: e.tensor_reduce(out=gkm[0:1, 0:1], in_=gkrow[0:1, 0:128], axis=mybir.AxisListType.X,
                                           op=ALU.max, apply_absolute_value=True), r=[bgkrow], w=[bgkm])
    S.add("dve", lambda e: e.tensor_reduce(out=gkm[0:1, 1:2], in_=gkrow[0:1, 128:256], axis=mybir.AxisListType.X,
                                           op=ALU.max, apply_absolute_value=True), r=[bgkrow], w=[bgkm])
    tt("dve", gkm[0:1, 2:3], gkm[0:1, 0:1], gkm[0:1, 1:2], ALU.mult, [bgkm], [bgkm])
    ts("dve", gkm[0:1, 3:4], gkm[0:1, 2:3], -float(np.sqrt(128.0)), None, ALU.mult, None, [bgkm], [bgkm])
    mm(PS[7][:, 0:1], onesf[0:1, :], gkm[0:1, 3:4], True, True, [bgkm, bconst], [bPS[7]])
    cp("dve", nbias[:], PS[7][:, 0:1], [bPS[7]], [bnbias])

    act(scT[:], smalls[:, SM_CV:SM_CV + 32], AF.Silu, [bsm], [bscT])

    class ModStream:
        def __init__(self, chunks, ring, npf, banks):
            self.chunks = chunks; self.ring = ring; self.npf = npf; self.loaded = 0; self.done = 0; self.banks = banks
            self.load_ops = []

        def _load(self):
            i = self.loaded
            c = self.chunks[i]
            tl, bf, key = self.ring[i % len(self.ring)]
            self.load_ops.append(dma("sp", tl, wmod_d[c * P:(c + 1) * P, :], [], [bf], key))
            self.loaded += 1

        def step(self, nsteps=1):
            for _ in range(nsteps):
                if self.done >= len(self.chunks):
                    return
                while self.loaded < min(len(self.chunks), self.done + self.npf + 1):
                    self._load()
                i = self.done
                c = self.chunks[i]
                tl, bf, key = self.ring[i % len(self.ring)]
                (pm, bpm), (ptp, bptp) = self.banks(i)
                st, bst = stg[i % 2], bstg[i % 2]
                for k in range(KC):
                    mm(pm, scT[:, 2 * k:2 * k + 2], tl[:, k * P:(k + 1) * P], k == 0, k == KC - 1, [bscT, bf], [bpm])
                act(st[0:2, 0:P], pm, AF.Copy, [bpm], [bst])
                dma("pool", modraw_d[0:2, c * P:(c + 1) * P], st[0:2, 0:P], [bst], [B.modraw], "modraw", nowaw=True)
                tr(ptp, st[0:2, 0:P], identf[0:2, 0:2], [bst, bcf], [bptp])
                ts("dve", modT[:, 2 * c:2 * c + 2], ptp, smalls[:, SM_BM + c:SM_BM + c + 1], None, ALU.add, None,
                   [bptp, bsm], [bmodT])
                self.done += 1

    def ABs(w_, i, ab):
        o = ((w_ * 3 + i) * 2 + ab) * 16
        return AB[:, o:o + 16]
    def modv(i_mod, w_):
        base = i_mod * 32 + w_
        return modT[:, base:base + 32:2]
    def make_AB(i):
        for w_ in range(2):
            ts("dve", ABs(w_, i, 0), modv(3 * i + 1, w_), 1.0, None, ALU.add, None, [bmodT], [bAB])
            tt("dve", ABs(w_, i, 0), ABs(w_, i, 0), smalls[:, SM_NG + 16 * i:SM_NG + 16 * i + 16], ALU.mult,
               [bAB, bsm], [bAB])
            cp("dve", ABs(w_, i, 1), modv(3 * i, w_), [bmodT], [bAB])

    grows_d = dscr("grows", [4, D], F32); B.grows = Buf("grows")
    GROWS = ((1, 2, 1, 0.5), (0, 2, 1, 0.5), (0, 5, 2, 1.0), (0, 8, 3, 0.5))
    def make_grow(gi_, ta, bta, ka, tb, btb, kb):
        w_, i_mod, rowi, scale = GROWS[gi_]
        dma("sp", ta[0:1, :], modraw_d[w_:w_ + 1, i_mod * D:(i_mod + 1) * D], [B.modraw], [bta], ka)
        dma("sp", tb[0:1, :], rows_d[rowi:rowi + 1, :], [], [btb], kb)
        tt("dve", ta[0:1, :], ta[0:1, :], tb[0:1, :], ALU.add, [bta, btb], [bta])
        if scale != 1.0:
            ts("dve", ta[0:1, :], ta[0:1, :], float(scale), None, ALU.mult, None, [bta], [bta])
        dma("pool", grows_d[gi_:gi_ + 1, :], ta[0:1, :], [bta], [B.grows], "growst")

    ms0 = ModStream(list(range(0, 80)), [(BT[i][:, 0:D], bBT[i], f"bt{i}") for i in range(8)], 6,
                    lambda i: ((PS[i % 2][0:2, 0:P], bPS[i % 2]), (PS[2 + i % 2][:, 0:2], bPS[2 + i % 2])))
    ep_i = 0
    for c_ in range(48):
        ms0.step(1)
        while ep_i < len(early_pieces) and ep_i < (c_ + 1) * 0.7:
            early_pieces[ep_i](extra=(ms0.load_ops[min(len(ms0.load_ops) - 1, c_ + 2)],))
            ep_i += 1
    while ep_i < len(early_pieces):
        early_pieces[ep_i]()
        ep_i += 1
    make_AB(0)
    make_grow(0, XN[0], bXN[0], "xn0", GT[1], bGT[1], "gt1")
    make_grow(1, XN[0], bXN[0], "xn0", GT[1], bGT[1], "gt1")
    ms0.step(32)
    make_AB(1)
    ms1 = ModStream(list(range(80, 144)), [(XN[0][:], bXN[0], "xn0"), (GT[0][:], bGT[0], "gt0")], 1,
                    lambda i: ((PS[7][0:2, 0:P], bPS[7]), (PS[7][:, 256:258], bPS[7])))

    def load_gate(gt_i, gi_):
        dma("sp", GT[gt_i][:], grows_d[gi_:gi_ + 1, :].partition_broadcast(P), [B.grows], [bGT[gt_i]], f"gt{gt_i}")

    plan = []
    def plan_ffn(f, ph):
        idx = []
        for j in range(NG):
            g_ = len(plan); plan.append((wffb[("g", f)][j * P:(j + 1) * P, :], getattr(B, f"cast_ff{f}"), ph))
            u_ = len(plan); plan.append((wffb[("u", f)][j * P:(j + 1) * P, :], getattr(B, f"cast_ff{f}"), ph))
            d_ = len(plan); plan.append((wffb[("d", f)][j * P:(j + 1) * P, :], getattr(B, f"cast_ff{f}"), ph))
            idx.append((g_, u_, d_))
        return idx
    def plan_win(groups, ph):
        idx = {}
        for g_ in groups:
            idx[g_] = len(plan); plan.append((winb[g_ * P:(g_ + 1) * P, :], B.cast_win, ph))
        return idx
    def plan_wout(ph):
        idx = []
        for g_ in range(8):
            idx.append(len(plan)); plan.append((woutb[g_ * P:(g_ + 1) * P, :], B.cast_wout, ph))
        return idx

    tiles1 = [("c", 0, CT // P)] + [("x", i, 4) for i in range(T // TT)]
    plan1a = [plan_ffn(0, 1) for _ in tiles1]
    plan1b = [plan_win(range(8, 20) if kind == "c" else range(44), 1) for (kind, ti, nsub) in tiles1]
    plan3 = []
    for ti in range(T // TT):
        wo = plan_wout(3)
        fi = plan_ffn(1, 3)
        plan3.append((wo, fi))

    wstate = {"next": 0}
    def wprefetch(oldest):
        ph = plan[oldest][2]
        lim = min(len(plan), oldest + NRING)
        while wstate["next"] < lim and plan[wstate["next"]][2] == ph:
            i = wstate["next"]
            src, cb_, _ = plan[i]
            dma("sp", WR[i % NRING][:, 0:4096], src, [cb_], [bWR[i % NRING]], f"wr{i % NRING}")
            wstate["next"] += 1
    def wget(n):
        assert wstate["next"] > n, (wstate, n)
        return WR[n % NRING], bWR[n % NRING]

    evq = {"i": 0}
    bssqc = [Buf(f"ssq{i}") for i in range(8)]
    brstdc = [Buf(f"rstd{i}") for i in range(8)]
    BT4bf = BT[4][:, 0:D].bitcast(BF16)
    BT5bf = BT[5][:, 0:D].bitcast(BF16)
    def HTv(bi, k, a, b):
        if bi == 0:
            return HT[:, k * 512 + a: k * 512 + b]
        t_ = BT4bf if k < 8 else BT5bf
        return t_[:, (k % 8) * 512 + a:(k % 8) * 512 + b]
    def HTb(bi):
        return bHT if bi == 0 else [bBT[4], bBT[5]]

    def norm_part0(src_ap, src_buf, s, xi):
        xn, bxn = XN[xi], bXN[xi]
        act(junk[:], src_ap, AF.Square, [src_buf], [bjunk, bssqc[s]], accum_out=ssq[:, s:s + 1])
        act(rstd[:, s:s + 1], ssq[:, s:s + 1], AF.Sqrt, [bssqc[s], bconst], [brstdc[s]], bias=epsc[:, 0:1], scale=1.0 / D)
        S.add("dve", lambda e: e.reciprocal(out=rstd[:, s:s + 1], in_=rstd[:, s:s + 1]), r=[brstdc[s]], w=[brstdc[s]])
        act(xn[:], src_ap, AF.Copy, [src_buf, brstdc[s]], [bxn], scale=rstd[:, s:s + 1])

    def norm_part1(s, xi, w_, i, hb=0):
        xn, bxn = XN[xi], bXN[xi]
        for kb in range(4):
            pt, bpt = PS[6 + kb % 2], bPS[6 + kb % 2]
            for j in range(4):
                k = kb * 4 + j
                tr(pt[:, j * P:(j + 1) * P], xn[:, k * P:(k + 1) * P], identf, [bxn, bcf], [bpt])
            for j in range(4):
                k = kb * 4 + j
                A_ = ABs(w_, i, 0)[:, k:k + 1]
                B_ = ABs(w_, i, 1)[:, k:k + 1]
                o = HTv(hb, k, s * P, (s + 1) * P)
                src = pt[:, j * P:(j + 1) * P]
                if evq["i"] % 3 != 2:
                    ts("dve", o, src, A_, B_, ALU.mult, ALU.add, [bpt, bAB], HTb(hb))
                else:
                    act(o, src, AF.Identity, [bpt, bAB], HTb(hb), scale=A_, bias=B_)
                evq["i"] += 1

    def norm_to_hT(src_tiles, src_bufs, nsub, w_, i, xn_list, hb=0):
        for s in range(nsub):
            xi = xn_list[s % len(xn_list)]
            norm_part0(src_tiles[s], src_bufs[s], s, xi)
            norm_part1(s, xi, w_, i, hb)

    def ffn(fidx, nsub, acc_tiles, acc_bufs, hook=None):
        ntok = nsub * P
        cnt = 0
        for j in range(NG):
            gi, ui, di = fidx[j]
            if hook is not None:
                hook(j)
            wprefetch(gi)
            wg_, bwg = wget(gi)
            wu_, bwu = wget(ui)
            wd_, bwd = wget(di)
            hd, bhd = hid[j % 2], bhid[j % 2]
            for fc in range(GF):
                pg, bpg = PS[2 * (fc % 2)], bPS[2 * (fc % 2)]
                pu, bpu = PS[2 * (fc % 2) + 1], bPS[2 * (fc % 2) + 1]
                for k in range(KC):
                    mm(pg[:, 0:ntok], wg_[:, k * 256 + fc * P:k * 256 + (fc + 1) * P], HT3(k, 0, ntok),
                       k == 0, k == KC - 1, [bwg] + bHT, [bpg])
                for k in range(KC):
                    mm(pu[:, 0:ntok], wu_[:, k * 256 + fc * P:k * 256 + (fc + 1) * P], HT3(k, 0, ntok),
                       k == 0, k == KC - 1, [bwu] + bHT, [bpu])
                sl, bsl = sil[fc % 2], bsil[fc % 2]
                act(sl[:, 0:ntok], pg[:, 0:ntok], AF.Silu, [bpg], [bsl])
                tt("dve", hd[:, fc * 512:fc * 512 + ntok], sl[:, 0:ntok], pu[:, 0:ntok], ALU.mult, [bsl, bpu], [bhd])
            wprefetch(di)
            for s in range(nsub):
                for dg in range(4):
                    pd, bpd = PS[4 + cnt % 4], bPS[4 + cnt % 4]
                    cnt += 1
                    for fc in range(GF):
                        mm(pd[:, :], hd[:, fc * 512 + s * P:fc * 512 + (s + 1) * P],
                           wd_[:, fc * D + dg * 512:fc * D + (dg + 1) * 512], fc == 0, fc == GF - 1,
                           [bhd, bwd], [bpd])
                    a_ = acc_tiles[s][:, dg * 512:(dg + 1) * 512]
                    if j == 0:
                        cp("dve", a_, pd[:, :], [bpd], [acc_bufs[s]])
                    else:
                        tt("dve", a_, a_, pd[:, :], ALU.add, [bpd, acc_bufs[s]], [acc_bufs[s]])

    xt = [BT[s][:, 0:D] for s in range(4)]
    bxt = [bBT[s] for s in range(4)]
    acc = [BT[4 + s][:, 0:D] for s in range(4)]
    bacc = [bBT[4 + s] for s in range(4)]
    stq = {"f": 0, "b": 0, "v": 0}

    def evac_store(pt_ap, bpt, ntok, func, dst_ap, dst_buf, bf):
        if bf:
            i = stq["b"] % 2; stq["b"] += 1
            st_, bst_ = stgb[i], bstgb[i]; key = f"stgb{i}"
        else:
            i = stq["f"] % 2; stq["f"] += 1
            st_, bst_ = stg[i], bstg[i]; key = f"stg{i}"
        act(st_[:, 0:ntok], pt_ap, func, [bpt], [bst_])
        dma("pool", dst_ap, st_[:, 0:ntok], [bst_], [dst_buf], key, nowaw=True)
        return st_, bst_

    sqb2 = [sqb, sb("sqb1", [P, TT], BF16)]; bsqb2 = [bsqb, Buf("sqb1")]
    rsd2 = [rsd, sb("rsd1", [P, TT], F32)]; brsd2 = [brsd, Buf("rsd1")]
    qnb2 = [qnb, sb("qnb1", [P, TT], BF16)]; bqnb2 = [bqnb, Buf("qnb1")]

    class QKPipe:
        def __init__(self):
            self.items = []
            self.s2 = 0
            self.s3 = 0
            self.fresh = False

        def push(self, pt, bpt, ntok, gidx, rope_on, dst_ap, dst_buf):
            c = len(self.items)
            self.items.append((pt, bpt, ntok, gidx, rope_on, dst_ap, dst_buf))
            self.fresh = True
            act(sqb2[c % 2][:, 0:ntok], pt[:, 0:ntok], AF.Square, [bpt], [bsqb2[c % 2]])

        def advance(self):
            if self.s3 < self.s2:
                self._stage3(self.s3)
                self.s3 += 1
            lim = len(self.items) - (1 if self.fresh else 0)
            if self.s2 < lim:
                self._stage2(self.s2)
                self.s2 += 1
            self.fresh = False

        def _stage2(self, c):
            pt, bpt, ntok, gidx, rope_on, dst_ap, dst_buf = self.items[c]
            p2, bp2 = PS[4 + c % 2], bPS[4 + c % 2]
            mm(p2[:, 0:ntok], onesb[:], sqb2[c % 2][:, 0:ntok], True, True, [bconst, bsqb2[c % 2]], [bp2])
            r_, br_ = rsd2[c % 2], brsd2[c % 2]
            act(r_[:, 0:ntok], p2[:, 0:ntok], AF.Ln, [bp2, bconst], [br_], bias=epsc[:, 0:1], scale=1.0 / 128.0)
            act(r_[:, 0:ntok], r_[:, 0:ntok], AF.Exp, [br_], [br_], scale=-0.5)
            gcol = smalls[:, SM_QG + gidx:SM_QG + gidx + 1]
            if rope_on:
                stt("dve", qnb2[c % 2][:, 0:ntok], pt[:, 0:ntok], gcol, r_[:, 0:ntok], ALU.mult, ALU.mult,
                    [bpt, bsm, br_], [bqnb2[c % 2]])
            else:
                i = stq["b"] % 2; stq["b"] += 1
                st_, bst_ = stgb[i], bstgb[i]
                stt("dve", st_[:, 0:ntok], pt[:, 0:ntok], gcol, r_[:, 0:ntok], ALU.mult, ALU.mult, [bpt, bsm, br_], [bst_])
                dma("pool", dst_ap, st_[:, 0:ntok], [bst_], [dst_buf], f"stgb{i}", nowaw=True)

        def _stage3(self, c):
            pt, bpt, ntok, gidx, rope_on, dst_ap, dst_buf = self.items[c]
            if not rope_on:
                return
            q_, bq_ = qnb2[c % 2], bqnb2[c % 2]
            p3, bp3 = PS[6 + c % 2], bPS[6 + c % 2]
            mm(p3[:, 0:ntok], permb[:], q_[:, 0:ntok], True, True, [bconst, bq_], [bp3])
            tt("pool", t1[:, 0:ntok], q_[:, 0:ntok], ropet[:, 0:ntok], ALU.mult, [bq_, bropet], [bt1])
            tt("dve", t2[:, 0:ntok], p3[:, 0:ntok], ropet[:, TT:TT + ntok], ALU.mult, [bp3, bropet], [bt2])
            i = stq["b"] % 2; stq["b"] += 1
            st_, bst_ = stgb[i], bstgb[i]
            tt("dve", st_[:, 0:ntok], t1[:, 0:ntok], t2[:, 0:ntok], ALU.add, [bt1, bt2], [bst_])
            dma("pool", dst_ap, st_[:, 0:ntok], [bst_], [dst_buf], f"stgb{i}", nowaw=True)

        def flush(self):
            self.fresh = False
            while self.s3 < len(self.items):
                self.advance()

    x1c_d = dscr("x1c", [CT, D], F32); B.x1c = Buf("x1c")

    for tix, (kind, ti, nsub) in enumerate(tiles1):
        ntok = nsub * P
        w_ = 1 if kind == "c" else 0
        fidx = plan1a[tix]
        src_d = ctx_d if kind == "c" else x_d
        r0 = 0 if kind == "c" else ti * TT
        if tix == 0:
            load_gate(0, 0)
        if tix == 1:
            load_gate(0, 1)
        for s in range(nsub):
            dma("sp", xt[s], src_d[r0 + s * P:r0 + (s + 1) * P, :], [], [bxt[s]], f"bt{s}")
        norm_to_hT(xt, bxt, nsub, w_, 0, [0, 1])
        def hook1a(j, tix=tix):
            if tix in (0, 1) and win_pieces:
                win_pieces.pop(0)()
        ffn(fidx, nsub, acc, bacc, hook=hook1a)
        if tix == 1:
            while win_pieces:
                win_pieces.pop(0)()
        for s in range(nsub):
            tt("dve", acc[s], acc[s], GT[0][:], ALU.mult, [bacc[s], bGT[0]], [bacc[s]])
            tt("dve", acc[s], acc[s], xt[s], ALU.add, [bxt[s], bacc[s]], [bacc[s]])
            if kind == "x":
                dma("pool", x1_d[r0 + s * P:r0 + (s + 1) * P, :], acc[s], [bacc[s]], [B.x1], f"x1st{s}", nowaw=True)
            else:
                dma("pool", x1c_d[s * P:(s + 1) * P, :], acc[s], [bacc[s]], [B.x1c], f"x1st{s}", nowaw=True)

    def load_x1_tile(tix):
        kind, ti, nsub = tiles1[tix]
        r0 = 0 if kind == "c" else ti * TT
        for s in range(nsub):
            if kind == "x":
                dma("sp", xt[s], x1_d[r0 + s * P:r0 + (s + 1) * P, :], [B.x1], [bxt[s]], f"bt{s}")
            else:
                dma("sp", xt[s], x1c_d[s * P:(s + 1) * P, :], [B.x1c], [bxt[s]], f"bt{s}")

    load_x1_tile(0)
    norm_to_hT(xt, bxt, tiles1[0][2], 1 if tiles1[0][0] == "c" else 0, 1, [0, 1], hb=0)
    for tix, (kind, ti, nsub) in enumerate(tiles1):
        ntok = nsub * P
        hb = tix % 2
        widx = plan1b[tix]
        r0 = 0 if kind == "c" else ti * TT
        if kind == "x":
            dma("sp", ropet[:, 0:TT], rope_d[:, r0:r0 + TT], [], [bropet], "ropet", nowaw=True)
            dma("sp", ropet[:, TT:2 * TT], rope_d[:, T + r0:T + r0 + TT], [], [bropet], "ropet", nowaw=True)
        pieces = []
        if tix + 1 < len(tiles1):
            kind2, ti2, nsub2 = tiles1[tix + 1]
            w2 = 1 if kind2 == "c" else 0
            pieces.append(lambda: load_x1_tile(tix + 1))
            for s2 in range(nsub2):
                pieces.append(lambda s2=s2: norm_part0(xt[s2], bxt[s2], s2, s2 % 2))
                pieces.append(lambda s2=s2, w2=w2, hb=hb: norm_part1(s2, s2 % 2, w2, 1, 1 - hb))
        nticks = sum((nsub if g_ in (10, 11) else 2) for g_ in widx)
        start = 30 if kind == "x" else 5
        step = max(1, (nticks - start - 2) // max(1, len(pieces)))
        tk = {"i": 0}
        def tick():
            tk["i"] += 1
            if pieces and tk["i"] >= start and (tk["i"] - start) % step == 0:
                pieces.pop(0)()
        pcount = 0
        qk = QKPipe()
        for g_ in sorted(widx.keys()):
            wprefetch(widx[g_])
            ws, bws = wget(widx[g_])
            if g_ in (10, 11):
                for s in range(nsub):
                    pv, bpv = PS[pcount % 4], bPS[pcount % 4]; pcount += 1
                    for k in range(KC):
                        mm(pv[:, 0:256], HTv(hb, k, s * P, (s + 1) * P), ws[:, k * 256:(k + 1) * 256], k == 0, k == KC - 1,
                           [bws] + HTb(hb), [bpv])
                    qk.advance()
                    i = stq["v"] % 2; stq["v"] += 1
                    cp("dve", vst[i][:], pv[:, 0:256], [bpv], [bvst[i]])
                    c0 = (g_ - 10) * 256
                    if kind == "c":
                        dma("pool", vc_d[s * P:(s + 1) * P, c0:c0 + 256], vst[i][:], [bvst[i]], [B.vc], f"vst{i}", nowaw=True)
                    else:
                        dma("pool", vo_d[r0 + s * P:r0 + (s + 1) * P, c0:c0 + 256], vst[i][:], [bvst[i]], [B.vo],
                            f"vst{i}", nowaw=True)
                    tick()
                continue
            for ci in range(2):
                cc = 2 * g_ + ci
                pt, bpt = PS[pcount % 4], bPS[pcount % 4]; pcount += 1
                for k in range(KC):
                    mm(pt[:, 0:ntok], ws[:, k * 256 + ci * P:k * 256 + (ci + 1) * P], HTv(hb, k, 0, ntok), k == 0, k == KC - 1,
                       [bws] + HTb(hb), [bpt])
                if cc < 16:
                    qk.push(pt, bpt, ntok, 0, True, qT_d[cc * P:(cc + 1) * P, r0:r0 + ntok], B.qT)
                elif cc < 20:
                    kv = cc - 16
                    if kind == "c":
                        qk.push(pt, bpt, ntok, 1, False, kTc_d[:, kv * CT:(kv + 1) * CT], B.kTc)
                    else:
                        qk.push(pt, bpt, ntok, 1, True, kTo_d[:, kv * T + r0:kv * T + r0 + ntok], B.kTo)
                qk.advance()
                if cc < 20:
                    pass
                elif cc < 40:
                    n = cc - 24
                    if kind == "c":
                        evac_store(pt[:, 0:ntok], bpt, ntok, AF.Copy, lxc_d[n * P:(n + 1) * P, 0:CT], B.lxc, False)
                    else:
                        st_, bst_ = evac_store(pt[:, 0:ntok], bpt, ntok, AF.Copy, lx_d[n * P:(n + 1) * P, r0:r0 + ntok],
                                               B.lx, False)
                        if ti == T // TT - 1:
                            cp("dve", edgt[:, 2 * n:2 * n + 1], st_[:, ntok - 1:ntok], [bst_], [bedgt])
                            cp("dve", edgt[:, 2 * n + 1:2 * n + 2], st_[:, ntok - 2:ntok - 1], [bst_], [bedgt])
                elif cc < 56:
                    n = cc - 40
                    evac_store(pt[:, 0:ntok], bpt, ntok, AF.Gelu, lg_d[n * P:(n + 1) * P, r0:r0 + ntok], B.lg, True)
                elif cc < 72:
                    n = cc - 56
                    evac_store(pt[:, 0:ntok], bpt, ntok, AF.Sigmoid, ga_d[n * P:(n + 1) * P, r0:r0 + ntok], B.ga, True)
                else:
                    n = cc - 72
                    evac_store(pt[:, 0:ntok], bpt, ntok, AF.Gelu if False else AF.Sigmoid, gl_d[n * P:(n + 1) * P, r0:r0 + ntok], B.gl, True)
                tick()
        qk.flush()
        while pieces:
            pieces.pop(0)()

    dma("pool", edg_d, edgt[:], [bedgt], [B.edg], "edgst")
    rg = [[2 * i, 2 * i + 1] for i in range(n_cores // 2)]
    def coll(ins_, outs_, r, w, key):
        S.add("pool", lambda e: e.collective_compute("AllGather", ALU.bypass, replica_groups=rg, ins=[ins_], outs=[outs_]),
              r=r, w=w, dma=key, inc=1)
    coll(kTo_d, kTa_d, [B.kTo], [B.kTa], "cc_k")
    coll(vo_d, va_d, [B.vo], [B.va], "cc_v")
    coll(edg_d, edga_d, [B.edg], [B.edga], "cc_e")
    dma("sp", eall[0][:], edga_d[0:P, :], [B.edga], [beall], "eall", nowaw=True)
    dma("sp", eall[1][:], edga_d[P:2 * P, :], [B.edga], [beall], "eall", nowaw=True)
    ts("dve", epart[:], eall[0][:], smalls[:, SM_FB:SM_FB + 1], None, ALU.mult, None, [beall, bsm], [bepart])
    stt("dve", epart[:], eall[1][:], smalls[:, SM_FA:SM_FA + 1], epart[:], ALU.mult, ALU.add, [beall, bsm, bepart], [bepart])

    lwq = {"i": 0}
    act(clt[:, 0:32], smalls[:, SM_LAM:SM_LAM + 32], AF.Exp, [bsm], [bclt], scale=-1.0)
    act(clt[:, 0:32], clt[:, 0:32], AF.Ln, [bclt, bconst], [bclt], bias=epsc[:, 1:2], scale=1.0)
    ts("dve", clt[:, 32:64], clt[:, 0:32], -8.0, None, ALU.mult, None, [bclt], [bclt])
    ts("dve", clt[:, 64:96], clt[:, 0:32], -16.0, None, ALU.mult, None, [bclt], [bclt])

    gbank = {"i": 0}
    def lru_load_w(n, dr, li):
        ma, mx = 2 * dr, 2 * dr + 1
        lruW, blruW = lruWt[li], blruWt[li]
        dma("sp", lruW[:, 0:P], lruwb[:, (ma * 16 + n) * P:(ma * 16 + n + 1) * P], [B.cast_lruw], [blruW], f"lruW{li}", nowaw=True)
        dma("sp", lruW[:, P:2 * P], lruwb[:, (mx * 16 + n) * P:(mx * 16 + n + 1) * P], [B.cast_lruw], [blruW], f"lruW{li}", nowaw=True)

    def lru_h2(n, dr, xc_ap, xc16_ap, bxc, bxc16, ntok, ra, bra, ix, bix, a2, ba2, li):
        ma, mx = 2 * dr, 2 * dr + 1
        lruW, blruW = lruWt[li], blruWt[li]
        for (m_, dst, bdst) in ((ma, ra, bra), (mx, ix, bix)):
            for t4 in range(0, ntok, 512):
                w4 = min(512, ntok - t4)
                gi_ = 6 + gbank["i"] % 2; gbank["i"] += 1
                pg, bpg = PS[gi_], bPS[gi_]
                mm(pg[:, 0:w4], lruW[:, (m_ % 2) * P:(m_ % 2 + 1) * P], xc16_ap[:, t4:t4 + w4], True, True,
                   [blruW, bxc16], [bpg])
                act(dst[:, t4:t4 + w4], pg[:, 0:w4], AF.Sigmoid, [bpg, bsm], [bdst],
                    bias=smalls[:, SM_LB + m_ * 16 + n:SM_LB + m_ * 16 + n + 1], scale=1.0)
        cl = clt[:, 32 + dr * 16 + n:32 + dr * 16 + n + 1]
        act(ra[:, 0:ntok], ra[:, 0:ntok], AF.Exp, [bra, bclt], [bra], scale=cl)
        tt("pool", a2[:, 0:ntok], ra[:, 0:ntok], ra[:, 0:ntok], ALU.mult, [bra], [ba2])
        ts("dve", a2[:, 0:ntok], a2[:, 0:ntok], -1.0, 1.0, ALU.mult, ALU.add, [ba2], [ba2])
        tt("dve", ix[:, 0:ntok], ix[:, 0:ntok], xc_ap[:, 0:ntok], ALU.mult, [bix, bxc], [bix])

    def lru_h3(ntok, ra, bra, ix, bix, a2, ba2, h_ap, bh, init_ap, binit, rev):
        act(a2[:, 0:ntok], a2[:, 0:ntok], AF.Sqrt, [ba2], [ba2])
        tt("dve", ix[:, 0:ntok], ix[:, 0:ntok], a2[:, 0:ntok], ALU.mult, [bix, ba2], [bix])
        if rev:
            S.add("dve", lambda e: e.tensor_tensor_scan(out=h_ap[:, 0:ntok][:, ::-1],
                                                        data0=ra[:, 0:ntok][:, ::-1], data1=ix[:, 0:ntok][:, ::-1],
                                                        initial=init_ap, op0=ALU.mult, op1=ALU.add),
                  r=[bra, bix, binit], w=[bh])
        else:
            S.add("dve", lambda e: e.tensor_tensor_scan(out=h_ap[:, 0:ntok], data0=ra[:, 0:ntok], data1=ix[:, 0:ntok],
                                                        initial=init_ap, op0=ALU.mult, op1=ALU.add),
                  r=[bra, bix, binit], w=[bh])

    def conv5(n, src, bsrc, ntok, xc, bxc):
        cw = lambda j: smalls[:, SM_CW + n * 5 + j:SM_CW + n * 5 + j + 1]
        ts("dve", xc[:, 0:ntok], src[:, 0:ntok], cw(0), smalls[:, SM_CB + n:SM_CB + n + 1], ALU.mult, ALU.add,
           [bsrc, bsm], [bxc])
        for j in range(1, 5):
            stt("dve", xc[:, 0:ntok], src[:, j:j + ntok], cw(j), xc[:, 0:ntok], ALU.mult, ALU.add, [bsrc, bsm, bxc], [bxc])

    LXP, bLXP = BT[0], bBT[0]
    XC, bXC = BT[1], bBT[1]
    RA, bRA = BT[2], bBT[2]
    IX, bIX = BT[3], bBT[3]
    A2, bA2 = BT[4], bBT[4]
    HH, bHH = BT[5], bBT[5]
    ATT, bATT = BT[6], bBT[6]
    HFL, bHFL = BT[7], bBT[7]
    XC16 = HT[:, 0:2048]; bXC16 = bHT[0]
    LGt = HT[:, 2048:4096]; bLGt = bHT[1]
    MGo = WR[4][:, 0:T]; bMGo = bWR[4]
    GAt = HT[:, 4096:6144]; bGAt = bHT[2]
    GLt = HT[:, 6144:8192]; bGLt = bHT[3]

    memset("dve", lxc[:, 0:2], 0.0, [blxc])
    memset("dve", lxc[:, CT + 2:CT + 4], 0.0, [blxc])
    memset("dve", LXP[:, 0:2], 0.0, [bLXP])
    memset("dve", hce[:], 0.0, [bhce])
    zero_col = sb("zero_col", [P, 1], F32); bzc = Buf("zc")
    memset("dve", zero_col[:], 0.0, [bzc])

    at_d = dscr("att", [KC * P, T], F32); B.att = Buf("att")
    ATL, bATL = GT[1], bGT[1]

    def lx_load_conv(n):
        dma("sp", LXP[:, 2:2 + T], lx_d[n * P:(n + 1) * P, :], [B.lx], [bLXP], "bt0")
        cp("dve", LXP[:, 2 + T:4 + T], epart[:, 2 * n:2 * n + 2], [bepart], [bLXP])
        conv5(n, LXP, bLXP, T, XC, bXC)
        cp("pool", XC16, XC[:, 0:T], [bXC], [bXC16])

    def s1_h1(n):
        lru_load_w(n, 0, n % 2)
        dma("sp", lxc[:, 2:2 + CT], lxc_d[n * P:(n + 1) * P, :], [B.lxc], [blxc], "lxct")
        conv5(n, lxc, blxc, CT, cxt[0], bcxt[0])
        cp("pool", cx16[:], cxt[0][:], [bcxt[0]], [bcx16])
        lx_load_conv(n)

    def s1_h2(n):
        lru_h2(n, 0, cxt[0], cx16, bcxt[0], bcx16, CT, cxt[1], bcxt[1], cxt[2], bcxt[2], cxt[3], bcxt[3], n % 2)
        lru_h2(n, 0, XC, XC16, bXC, bXC16, T, RA, bRA, IX, bIX, A2, bA2, n % 2)

    def s1_h3(n):
        lru_h3(CT, cxt[1], bcxt[1], cxt[2], bcxt[2], cxt[3], bcxt[3], cxt[4], bcxt[4], zero_col[:, 0:1], bzc, False)
        cp("dve", hce[:, n:n + 1], cxt[4][:, CT - 1:CT], [bcxt[4]], [bhce])
        lru_h3(T, RA, bRA, IX, bIX, A2, bA2, HH, bHH, hce[:, n:n + 1], bhce, False)
        cp("dve", estt[:, n:n + 1], HH[:, T - 1:T], [bHH], [bestt])
        dma("pool", hf_d[n * P:(n + 1) * P, :], HH[:, 0:T], [bHH], [B.hf], "hfst", nowaw=True)

    def exchange_states():
        dma("pool", est_d, estt[:], [bestt], [B.est], "estst")
        coll(est_d, esta_d, [B.est], [B.esta], "cc_s")
        dma("sp", sall[0][:], esta_d[0:P, :], [B.esta], [bsall], "sall", nowaw=True)
        dma("sp", sall[1][:], esta_d[P:2 * P, :], [B.esta], [bsall], "sall", nowaw=True)
        ts("dve", spart[:], sall[0][:], smalls[:, SM_FB:SM_FB + 1], None, ALU.mult, None, [bsall, bsm], [bspart])
        stt("dve", spart[:], sall[1][:], smalls[:, SM_FA:SM_FA + 1], spart[:], ALU.mult, ALU.add,
            [bsall, bsm, bspart], [bspart])

    def m_h2(n):
        lru_h2(n, 1, XC, XC16, bXC, bXC16, T, RA, bRA, IX, bIX, A2, bA2, (n + 1) % 2)
        dma("sp", HFL[:, 0:T], hf_d[n * P:(n + 1) * P, :], [B.hf], [bHFL], "bt7")
        dma("sp", ATL[:], at_d[n * P:(n + 1) * P, :], [B.att], [bATL], "gt1")
        dma("sp", LGt, lg_d[n * P:(n + 1) * P, :], [B.lg], [bLGt], "ht1")
        dma("sp", GAt, ga_d[n * P:(n + 1) * P, :], [B.ga], [bGAt], "ht2")
        dma("sp", GLt, gl_d[n * P:(n + 1) * P, :], [B.gl], [bGLt], "ht3")

    def m_h3(n):
        lru_h3(T, RA, bRA, IX, bIX, A2, bA2, HH, bHH, spart[:, n:n + 1], bspart, True)
        tt("dve", HH[:, 0:T], HH[:, 0:T], HFL[:, 0:T], ALU.add, [bHH, bHFL], [bHH])
        tt("pool", HH[:, 0:T], HH[:, 0:T], LGt, ALU.mult, [bHH, bLGt], [bHH])
        tt("pool", HH[:, 0:T], HH[:, 0:T], GLt, ALU.mult, [bHH, bGLt], [bHH])
        tt("dve", ATL[:], ATL[:], GAt, ALU.mult, [bATL, bGAt], [bATL])
        tt("dve", MGo, ATL[:], HH[:, 0:T], ALU.add, [bATL, bHH], [bMGo])
        dma("pool", mg_d[n * P:(n + 1) * P, :], MGo, [bMGo], [B.mg], "mgst", nowaw=True)

    def mod_finish():
        make_AB(2)
        make_grow(2, XN[0], bXN[0], "xn0", GT[0], bGT[0], "gt0")
        make_grow(3, XN[0], bXN[0], "xn0", GT[0], bGT[0], "gt0")

    NBLK = NQ * (T // TT)
    jobs = []
    for n in range(KC):
        jobs.append((lambda n=n: s1_h1(n), lambda n=n: s1_h2(n), lambda n=n: s1_h3(n)))
    jobs.append((None, None, exchange_states))
    for n in range(KC):
        jobs.append((lambda n=n: (lru_load_w(n, 1, (n + 1) % 2), lx_load_conv(n)), lambda n=n: m_h2(n), lambda n=n: m_h3(n)))
    def blk_of(t):
        return min(NBLK, t if t < KC + 2 else max(t, 4 * (t - (KC + 2)) + 4))
    sched2 = {i: [] for i in range(NBLK + 1)}
    for t in range(len(jobs) + 2):
        def slot(t=t):
            for st_i, j in ((2, t - 2), (1, t - 1), (0, t)):
                if 0 <= j < len(jobs) and jobs[j][st_i] is not None:
                    jobs[j][st_i]()
        sched2[blk_of(t)].append(slot)
    for b_ in range(NBLK):
        sched2[b_].append(lambda: ms1.step(1))
        if b_ < len(late_pieces):
            sched2[b_].append(late_pieces[b_])
    assert len(late_pieces) <= NBLK
    sched2[NBLK].append(mod_finish)

    def qload(blk_):
        n_, q4_ = blk_ // (T // TT), blk_ % (T // TT)
        qi_ = blk_ % 2
        dma("sp", qTt[qi_][:], qT_d[n_ * P:(n_ + 1) * P, q4_ * TT:(q4_ + 1) * TT], [B.qT], [bqTt[qi_]], f"qTt{qi_}")

    KT = [WR[0], WR[1]]; bKT = [bWR[0], bWR[1]]
    VT = [WR[2], WR[3]]; bVT = [bWR[2], bWR[3]]
    for n in range(NQ):
        kv = n // 4
        kt_, bkt_ = KT[kv % 2], bKT[kv % 2]
        vt_, bvt_ = VT[kv % 2], bVT[kv % 2]
        if n % 4 == 0:
            key = f"wr{kv % 2}"
            dma("sp", kt_[:, 0:CT], kTc_d[:, kv * CT:(kv + 1) * CT], [B.kTc], [bkt_], key, nowaw=True)
            dma("sp", kt_[:, CT:CT + T], kTa_d[0:P, kv * T:(kv + 1) * T], [B.kTa], [bkt_], key, nowaw=True)
            dma("sp", kt_[:, CT + T:CT + 2 * T], kTa_d[P:2 * P, kv * T:(kv + 1) * T], [B.kTa], [bkt_], key, nowaw=True)
            key = f"wr{2 + kv % 2}"
            for t_ in range(NKT):
                if t_ < 2:
                    src = vc_d[t_ * P:(t_ + 1) * P, kv * P:(kv + 1) * P]; bsrc = B.vc
                else:
                    src = va_d[(t_ - 2) * P:(t_ - 1) * P, kv * P:(kv + 1) * P]; bsrc = B.va
                dma("sp", vt_[:, t_ * P:(t_ + 1) * P], src, [bsrc], [bvt_], key, nowaw=True)
        for q4 in range(T // TT):
            po, bpo = PS[3 + q4 % 2], bPS[3 + q4 % 2]
            pd_, bpd_ = PS[5], bPS[5]
            blk_ = n * 4 + q4
            if blk_ == 0:
                qload(0)
            if blk_ + 1 < NQ * (T // TT):
                qload(blk_ + 1)
            qi_ = blk_ % 2
            qt_, bqt_ = qTt[qi_], bqTt[qi_]
            qs = qt_[:]

            def s_mm(t_):
                ps_, bps_ = PS[t_ % 3], bPS[t_ % 3]
                mm(ps_[:, :], kt_[:, t_ * P:(t_ + 1) * P], qs, True, True, [bkt_, bqt_], [bps_])

            def pv_mm(t_):
                ps_, bps_ = PS[t_ % 3], bPS[t_ % 3]
                pt_, bpt_ = pT[t_ % 3], bpT[t_ % 3]
                act(pt_[:], ps_[:, :], AF.Exp, [bps_, bnbias], [bpt_], bias=nbias[:, 0:1], scale=SM_SCALE)
                mm(po[:, :], vt_[:, t_ * P:(t_ + 1) * P], pt_[:], t_ == 0, t_ == NKT - 1, [bvt_, bpt_], [bpo])
                mm(pd_[:, :], onesb[:], pt_[:], t_ == 0, t_ == NKT - 1, [bconst, bpt_], [bpd_])

            s_mm(0)
            s_mm(1)
            for t_ in range(NKT):
                if t_ + 2 < NKT:
                    s_mm(t_ + 2)
                pv_mm(t_)
            cp("dve", rec[:], pd_[:, :], [bpd_], [brec])
            S.add("dve", lambda e: e.reciprocal(out=rec[:], in_=rec[:]), r=[brec], w=[brec])
            tt("dve", ATT[:, q4 * TT:(q4 + 1) * TT], po[:, :], rec[:], ALU.mult, [bpo, brec], [bATT])
            if q4 == T // TT - 1:
                dma("pool", at_d[n * P:(n + 1) * P, :], ATT[:, 0:T], [bATT], [B.att], "attst", nowaw=True)
            for task in sched2[n * 4 + q4]:
                task()
    for task in sched2[NBLK]:
        task()

    load_gate(1, 3)
    FG, bFG = GT[0], bGT[0]
    for ti in range(T // TT):
        r0 = ti * TT
        wo_idx, fidx = plan3[ti]
        for s in range(4):
            dma("sp", xt[s], x1_d[r0 + s * P:r0 + (s + 1) * P, :], [B.x1], [bxt[s]], f"bt{s}")
        load_gate(0, 2)
        for k in range(KC):
            dma("sp", HT3(k, 0, TT), mg_d[k * P:(k + 1) * P, r0:r0 + TT], [B.mg], bHT, "htld", nowaw=True)
        cnt = 0
        for g_ in range(8):
            wprefetch(wo_idx[g_])
            ws, bws = wget(wo_idx[g_])
            for s in range(4):
                pw, bpw = PS[4 + cnt % 2], bPS[4 + cnt % 2]; cnt += 1
                for k in range(KC):
                    mm(pw[:, 0:256], HT3(k, s * P, (s + 1) * P), ws[:, k * 256:(k + 1) * 256], k == 0, k == KC - 1,
                       [bws] + bHT, [bpw])
                tt("dve", tmpw[:], pw[:, 0:256], GT[0][:, g_ * 256:(g_ + 1) * 256], ALU.mult, [bpw, bGT[0]], [btmpw])
                tt("dve", xt[s][:, g_ * 256:(g_ + 1) * 256], xt[s][:, g_ * 256:(g_ + 1) * 256], tmpw[:], ALU.add,
                   [bxt[s], btmpw], [bxt[s]])
        norm_to_hT(xt, bxt, 4, 0, 2, [0, 1])
        ffn(fidx, 4, acc, bacc)
        dma("sp", FG[:], rows_d[0:1, :].partition_broadcast(P), [], [bFG], "gt0")
        for s in range(4):
            tt("dve", acc[s], acc[s], GT[1][:], ALU.mult, [bacc[s], bGT[1]], [bacc[s]])
            tt("dve", xt[s], xt[s], acc[s], ALU.add, [bxt[s], bacc[s]], [bxt[s]])
            act(junk[:], xt[s], AF.Square, [bxt[s]], [bjunk, bssqc[4 + s]], accum_out=ssq[:, 4 + s:5 + s])
            act(rstd[:, 4 + s:5 + s], ssq[:, 4 + s:5 + s], AF.Sqrt, [bssqc[4 + s], bconst], [brstdc[4 + s]], bias=epsc[:, 0:1],
                scale=1.0 / D)
            S.add("dve", lambda e, s=s: e.reciprocal(out=rstd[:, 4 + s:5 + s], in_=rstd[:, 4 + s:5 + s]),
                  r=[brstdc[4 + s]], w=[brstdc[4 + s]])
            stt("dve", acc[s], xt[s], rstd[:, 4 + s:5 + s], FG[:], ALU.mult, ALU.mult, [bxt[s], brstdc[4 + s], bFG], [bacc[s]])
            dma("pool", out_d[r0 + s * P:r0 + (s + 1) * P, :], acc[s], [bacc[s]], [B.out], f"ost{s}", nowaw=True)

    fin_reads = [B.out]
    if debug:
        for nm, (dst, src) in dbg_out.items():
            bsrc = getattr(B, nm[2:])
            bd = Buf(nm)
            dma("pool", dst, src, [bsrc], [bd], "dbg", nowaw=True)
            fin_reads.append(bd)
    S.add("pool", lambda e: e.nop(), r=fin_reads, w=[])

    S.finalize()
    esem = {e: es.enter_context(nc.semaphore(f"s_{e}")) for e in Sched.ENG}
    dsem = {k: es.enter_context(nc.semaphore(f"d_{k}")) for k in S.dma_count}
    block = es.enter_context(nc.Block())

    @block.tensor
    def _(e):
        S.emit("pe", e, esem, dsem)

    @block.scalar
    def _(e):
        S.emit("act", e, esem, dsem)

    @block.vector
    def _(e):
        S.emit("dve", e, esem, dsem)

    @block.gpsimd
    def _(e):
        S.emit("pool", e, esem, dsem)

    @block.sync
    def _(e):
        S.emit("sp", e, esem, dsem)

    es.close()
    return nc


def _rope_tables(pos):
    f32 = np.float32
    row = (pos // 64).astype(f32)
    col = (pos % 64).astype(f32)
    freqs = (f32(10000.0) ** (-(np.arange(0, 64, 2, dtype=f32)) / f32(64))).astype(f32)
    ang = np.concatenate([row[:, None] * freqs, col[:, None] * freqs], axis=-1).astype(f32)
    cos = np.cos(ang).astype(f32)
    sin = np.sin(ang).astype(f32)
    cosT = np.repeat(cos.T, 2, axis=0)
    sinT = np.repeat(sin.T, 2, axis=0).copy()
    sinT[0::2, :] *= -1.0
    return np.ascontiguousarray(np.concatenate([cosT, sinT], axis=1).astype(f32))


def _tile_w(w, gw):
    n = w.shape[1]
    return np.ascontiguousarray(w.reshape(16, 128, n // gw, gw).transpose(2, 1, 0, 3)).reshape((n // gw) * 128, 16 * gw)


def prepare(inputs, cores):
    f32 = np.float32
    g = {k: np.asarray(v) for k, v in inputs.items()}
    shared = {}
    shared["wmod"] = _tile_w(g["w_mod"][0], 128)
    for f in range(2):
        shared[f"wg{f}"] = _tile_w(g["ffn_wg"][0, f], 256)
        shared[f"wu{f}"] = _tile_w(g["ffn_wu"][0, f], 256)
        wd = g["ffn_wd"][0, f]
        shared[f"wd{f}"] = np.ascontiguousarray(wd.reshape(NG, GF, 128, D).transpose(0, 2, 1, 3)).reshape(NG * 128, GF * D)
    shared["win"] = _tile_w(g["w_in"][0], 256)
    shared["wout"] = _tile_w(g["w_out"][0], 256)
    perm = np.zeros((128, 128), f32)
    for d in range(128):
        perm[d ^ 1, d] = 1.0
    shared["consts"] = np.ascontiguousarray(np.concatenate([np.eye(128, dtype=f32), perm], axis=1))
    shared["gkrow"] = np.ascontiguousarray(np.concatenate([g["q_norm_g"][0], g["k_norm_g"][0]])[None, :].astype(f32))
    bm = g["b_mod"][0]
    shared["rows"] = np.ascontiguousarray(np.stack([g["final_norm_g"], bm[2 * D:3 * D], bm[5 * D:6 * D], bm[8 * D:9 * D]]).astype(f32))

    def colmajor(v):
        v = np.asarray(v, f32).reshape(-1, 16, 128)
        return v.transpose(2, 0, 1)

    in_maps = []
    for c in cores:
        b, half = c // 2, c % 2
        m = dict(shared)
        xs = g["x"][b, half * T:(half + 1) * T]
        cs = g["ctx"][b]
        pos = np.arange(half * T, (half + 1) * T)
        dirs = (0, 1)
        taps = g["conv_w"][0]
        w5 = np.zeros((5, D), f32)
        if half == 0:
            w5[0:4] = taps
        else:
            xs = xs[::-1]
            cs = cs[::-1]
            pos = pos[::-1]
            dirs = (1, 0)
            w5[1] = taps[3]; w5[2] = taps[2]; w5[3] = taps[1]; w5[4] = taps[0]
        m["x"] = np.ascontiguousarray(xs, dtype=f32)
        m["ctx"] = np.ascontiguousarray(cs, dtype=f32)
        m["rope"] = _rope_tables(pos)
        sm = np.zeros((128, NSM), f32)
        cv = np.stack([g["c"][b], g["c_ctx"]])
        sm[:, SM_CV:SM_CV + 32] = colmajor(cv).transpose(0, 2, 1).reshape(128, 32)
        sm[:, SM_NG:SM_NG + 48] = colmajor(g["norm_g"][0]).reshape(128, 48)
        sm[:, SM_BM:SM_BM + 144] = bm.reshape(144, 128).T
        lb = np.stack([g["lru_ba"][0, dirs[0]], g["lru_bx"][0, dirs[0]], g["lru_ba"][0, dirs[1]], g["lru_bx"][0, dirs[1]]])
        sm[:, SM_LB:SM_LB + 64] = colmajor(lb).reshape(128, 64)
        lam = np.stack([g["lru_lambda"][0, dirs[0]], g["lru_lambda"][0, dirs[1]]])
        sm[:, SM_LAM:SM_LAM + 32] = colmajor(lam).reshape(128, 32)
        sm[:, SM_CW:SM_CW + 80] = colmajor(w5).transpose(0, 2, 1).reshape(128, 80)
        sm[:, SM_CB:SM_CB + 16] = colmajor(g["conv_b"][0]).reshape(128, 16)
        sm[:, SM_QG] = g["q_norm_g"][0]
        sm[:, SM_KG] = g["k_norm_g"][0]
        sm[:, SM_FA] = 1.0 if half == 0 else 0.0
        sm[:, SM_FB] = 0.0 if half == 0 else 1.0
        m["smalls"] = sm
        lw = np.stack([g["lru_wa"][0, dirs[0]], g["lru_wx"][0, dirs[0]], g["lru_wa"][0, dirs[1]], g["lru_wx"][0, dirs[1]]])
        m["lruw"] = np.ascontiguousarray(lw.reshape(64, 128, 128).transpose(1, 0, 2)).reshape(128, 64 * 128).astype(f32)
        in_maps.append(m)
    return in_maps


_NC_CACHE = {}


def kernel(**inputs):
    cores = list(range(8))
    in_maps = prepare(inputs, cores)
    if "nc" not in _NC_CACHE:
        _NC_CACHE["nc"] = build(8, False)
    res = run_bass_kernel_spmd(_NC_CACHE["nc"], in_maps, core_ids=cores)
    out = np.empty((4, 4096, D), np.float32)
    for c in cores:
        b, half = c // 2, c % 2
        o = np.asarray(res.results[c]["out"])
        if half == 1:
            o = o[::-1]
        out[b, half * T:(half + 1) * T] = o
    return out
```

```python
import numpy as np
from contextlib import ExitStack
import concourse.bass as bass
import concourse.mybir as mybir
from concourse.bass_utils import run_bass_kernel_spmd

F32 = mybir.dt.float32
BF16 = mybir.dt.bfloat16
ALU = mybir.AluOpType
AF = mybir.ActivationFunctionType

P = 128
D = 2048
KC = 16
DFF = 5632
GF = 2
NG = DFF // (128 * GF)
T = 2048
TT = 512
CT = 256
NQ = 16
NKV = 4
EPS = 1e-6
NMOD = 9
NKT = (CT + 2 * T) // 128
SM_SCALE = 128 ** -0.5

SM_CV = 0
SM_NG = 32
SM_BM = 80
SM_LB = 224
SM_LAM = 288
SM_CW = 320
SM_CB = 400
SM_QG = 416
SM_KG = 417
SM_FA = 418
SM_FB = 419
NSM = 420


class Buf:
    __slots__ = ("name", "w", "r")

    def __init__(self, name):
        self.name = name
        self.w = None
        self.r = []


class Op:
    __slots__ = ("eng", "fn", "deps", "sig", "tick", "dma", "idx", "waits", "inc")


class Sched:
    ENG = ("pe", "act", "dve", "pool", "sp")

    def __init__(self):
        self.ops = {e: [] for e in self.ENG}
        self.dma_count = {}
        self.n = 0

    def add(self, eng, fn, r=(), w=(), dma=None, inc=16, nowaw=False, extra=()):
        op = Op()
        op.eng = eng
        op.fn = fn
        op.dma = dma
        op.sig = False
        op.inc = inc
        op.idx = self.n
        self.n += 1
        op.tick = 0
        if dma is not None:
            c = self.dma_count.get(dma, 0) + inc
            self.dma_count[dma] = c
            op.tick = c
        deps = set()
        for b in r:
            if b.w is not None:
                deps.add(b.w)
        for b in w:
            if b.w is not None and not (nowaw and b.w.dma is not None and b.w.dma == dma):
                deps.add(b.w)
            for o in b.r:
                deps.add(o)
        for b in r:
            b.r.append(op)
        for b in w:
            if nowaw and b.w is not None and b.w.dma == dma:
                pass
            b.w = op
            b.r = []
        for o in extra:
            deps.add(o)
        deps.discard(op)
        op.deps = deps
        self.ops[eng].append(op)
        return op

    def finalize(self):
        for e in self.ENG:
            for op in self.ops[e]:
                need = {}
                for d in op.deps:
                    if d.dma is not None:
                        key = ("dma", d.dma)
                        if key not in need or d.tick > need[key].tick:
                            need[key] = d
                    else:
                        if d.eng == "pe" and op.eng == "pe" and op.dma is None:
                            continue
                        key = d.eng
                        if key not in need or d.idx > need[key].idx:
                            need[key] = d
                op.waits = list(need.values())
                for d in op.waits:
                    if d.dma is None:
                        d.sig = True
        for e in self.ENG:
            c = 0
            for op in self.ops[e]:
                if op.dma is None and op.sig:
                    c += 1
                    op.tick = c

    def emit(self, ename, eng, esem, dsem):
        waited = {}
        for op in self.ops[ename]:
            for d in op.waits:
                if d.dma is not None:
                    key = ("dma", d.dma)
                    sem = dsem[d.dma]
                else:
                    key = d.eng
                    sem = esem[d.eng]
                if waited.get(key, 0) < d.tick:
                    eng.wait_ge(sem, d.tick)
                    waited[key] = d.tick
            ins = op.fn(eng)
            if op.dma is not None:
                ins.then_inc(dsem[op.dma], op.inc)
            elif op.sig:
                ins.then_inc(esem[ename], 1)


def build(n_cores=8, debug=False):
    nc = bass.Bass("TRN2", target_bir_lowering=False)
    S = Sched()
    es = ExitStack()

    def din(name, shape, dt=F32):
        return nc.dram_tensor(name, list(shape), dt, kind="ExternalInput").ap()

    def dscr(name, shape, dt):
        return nc.dram_tensor(name, list(shape), dt).ap()

    x_d = din("x", [T, D])
    ctx_d = din("ctx", [CT, D])
    smalls_d = din("smalls", [P, NSM])
    rows_d = din("rows", [4, D])
    gkrow_d = din("gkrow", [1, 256])
    wmod_d = din("wmod", [144 * P, D])
    wff_d = {}
    for f in range(2):
        wff_d[("g", f)] = din(f"wg{f}", [NG * P, 4096])
        wff_d[("u", f)] = din(f"wu{f}", [NG * P, 4096])
        wff_d[("d", f)] = din(f"wd{f}", [NG * P, 4096])
    win_d = din("win", [44 * P, 4096])
    wout_d = din("wout", [8 * P, 4096])
    lruw_d = din("lruw", [P, 64 * P])
    rope_d = din("rope", [P, 2 * T])
    consts_d = din("consts", [P, 2 * P])
    out_d = nc.dram_tensor("out", [T, D], F32, kind="ExternalOutput").ap()

    wffb = {k: dscr(f"b_{k[0]}{k[1]}", [NG * P, 4096], BF16) for k in wff_d}
    winb = dscr("b_win", [44 * P, 4096], BF16)
    woutb = dscr("b_wout", [8 * P, 4096], BF16)
    lruwb = dscr("b_lruw", [P, 64 * P], BF16)
    x1_d = dscr("x1", [T, D], F32)
    qT_d = dscr("qT", [NQ * P, T], BF16)
    kTo_d = dscr("kTo", [P, NKV * T], BF16)
    kTc_d = dscr("kTc", [P, NKV * CT], BF16)
    vo_d = dscr("vo", [T, 512], BF16)
    vc_d = dscr("vc", [CT, 512], BF16)
    kTa_d = dscr("kTa", [2 * P, NKV * T], BF16)
    va_d = dscr("va", [2 * T, 512], BF16)
    lx_d = dscr("lx", [KC * P, T], F32)
    lxc_d = dscr("lxc", [KC * P, CT], F32)
    lg_d = dscr("lg", [KC * P, T], BF16)
    ga_d = dscr("ga", [KC * P, T], BF16)
    gl_d = dscr("gl", [KC * P, T], BF16)
    hf_d = dscr("hf", [KC * P, T], F32)
    mg_d = dscr("mg", [KC * P, T], BF16)
    modraw_d = dscr("modraw", [2, NMOD * D], F32)
    edg_d = dscr("edg", [P, 32], F32)
    edga_d = dscr("edga", [2 * P, 32], F32)
    est_d = dscr("est", [P, 16], F32)
    esta_d = dscr("esta", [2 * P, 16], F32)

    dbg_out = {}
    if debug:
        for nm, src, shp, dt in (("d_x1", x1_d, [T, D], F32), ("d_qT", qT_d, [NQ * P, T], BF16),
                                 ("d_kTa", kTa_d, [2 * P, NKV * T], BF16), ("d_va", va_d, [2 * T, 512], BF16),
                                 ("d_kTc", kTc_d, [P, NKV * CT], BF16),
                                 ("d_lx", lx_d, [KC * P, T], F32), ("d_lg", lg_d, [KC * P, T], BF16),
                                 ("d_hf", hf_d, [KC * P, T], F32), ("d_ga", ga_d, [KC * P, T], BF16), ("d_gl", gl_d, [KC * P, T], BF16), ("d_mg", mg_d, [KC * P, T], BF16),
                                 ("d_modraw", modraw_d, [2, NMOD * D], F32), ("d_esta", esta_d, [2 * P, 16], F32)):
            dbg_out[nm] = (nc.dram_tensor(nm, shp, dt, kind="ExternalOutput").ap(), src)

    class NS_:
        pass
    B = NS_()
    for nm in ("cast_ff0", "cast_ff1", "cast_win", "cast_wout", "cast_lruw", "x1", "qT", "kTo", "kTc", "vo", "vc",
               "kTa", "va", "lx", "lxc", "lg", "ga", "gl", "hf", "mg", "modraw", "edg", "edga", "est", "esta", "out"):
        setattr(B, nm, Buf(nm))

    def sb(name, shape, dt):
        return es.enter_context(nc.sbuf_tensor("s_" + name, list(shape), dt))

    BT = [sb(f"bt{i}", [P, 2056], F32) for i in range(8)]
    bBT = [Buf(f"bt{i}") for i in range(8)]
    HT = sb("ht", [P, 8192], BF16)
    bHT = [Buf(f"ht{i}") for i in range(4)]
    NRING = 5
    WR = [sb(f"wr{i}", [P, 4352], BF16) for i in range(NRING)]
    bWR = [Buf(f"wr{i}") for i in range(NRING)]
    XN = [sb(f"xn{i}", [P, D], F32) for i in range(2)]
    bXN = [Buf(f"xn{i}") for i in range(2)]
    junk = sb("junk", [P, D], BF16); bjunk = Buf("junk")
    GT = [sb(f"gt{i}", [P, D], F32) for i in range(2)]
    bGT = [Buf(f"gt{i}") for i in range(2)]
    smalls = sb("smalls", [P, NSM], F32); bsm = Buf("smalls")
    constsf = sb("constsf", [P, 2 * P], F32); bcf = Buf("constsf")
    identb = sb("identb", [P, P], BF16)
    permb = sb("permb", [P, P], BF16)
    onesb = sb("onesb", [P, P], BF16)
    onesf = sb("onesf", [P, P], F32)
    bconst = Buf("constb")
    epsc = sb("epsc", [P, 2], F32)
    scT = sb("scT", [P, 32], F32); bscT = Buf("scT")
    modT = sb("modT", [P, 144 * 2], F32); bmodT = Buf("modT")
    AB = sb("AB", [P, 2 * 3 * 2 * 16], F32); bAB = Buf("AB")
    stg = [sb(f"stg{i}", [P, 512], F32) for i in range(2)]; bstg = [Buf(f"stg{i}") for i in range(2)]
    stgb = [sb(f"stgb{i}", [P, 512], BF16) for i in range(2)]; bstgb = [Buf(f"stgb{i}") for i in range(2)]
    hid = [sb(f"hid{i}", [P, GF * 512], BF16) for i in range(2)]; bhid = [Buf(f"hid{i}") for i in range(2)]
    sil = [sb(f"sil{i}", [P, 512], F32) for i in range(2)]; bsil = [Buf(f"sil{i}") for i in range(2)]
    ssq = sb("ssq", [P, 8], F32); bssq = Buf("ssq")
    rstd = sb("rstd", [P, 8], F32); brstd = Buf("rstd")
    ropet = sb("ropet", [P, 2 * TT], F32); bropet = Buf("ropet")
    sqb = sb("sqb", [P, TT], BF16); bsqb = Buf("sqb")
    rsd = sb("rsd", [P, TT], F32); brsd = Buf("rsd")
    qnb = sb("qnb", [P, TT], BF16); bqnb = Buf("qnb")
    t1, bt1 = sil[0], bsil[0]
    t2, bt2 = sil[1], bsil[1]
    vst = [sb(f"vst{i}", [P, 256], BF16) for i in range(2)]; bvst = [Buf(f"vst{i}") for i in range(2)]
    qTt = [sb(f"qTt{i}", [P, TT], BF16) for i in range(2)]; bqTt = [Buf(f"qTt{i}") for i in range(2)]
    pT = [sb(f"pT{i}", [P, TT], BF16) for i in range(3)]; bpT = [Buf(f"pT{i}") for i in range(3)]
    rec, brec = rsd, brsd
    nbias = sb("nbias", [P, 1], F32); bnbias = Buf("nbias")
    gkrow, bgkrow = stg[0], bstg[0]
    gkm = sb("gkm", [1, 4], F32); bgkm = Buf("gkm")
    lruWt = [sb(f"lruW{i}", [P, 2 * P], BF16) for i in range(2)]; blruWt = [Buf(f"lruW{i}") for i in range(2)]
    clt = sb("clt", [P, 96], F32); bclt = Buf("clt")
    lxc = sb("lxc", [P, CT + 4], F32); blxc = Buf("lxct")
    cxt = [sb(f"cxt{i}", [P, CT], F32) for i in range(5)]; bcxt = [Buf(f"cxt{i}") for i in range(5)]
    cx16 = sb("cx16", [P, CT], BF16); bcx16 = Buf("cx16")
    hce = sb("hce", [P, 16], F32); bhce = Buf("hce")
    estt = sb("estt", [P, 16], F32); bestt = Buf("estt")
    edgt = sb("edgt", [P, 32], F32); bedgt = Buf("edgt")
    eall = [sb(f"eall{i}", [P, 32], F32) for i in range(2)]; beall = Buf("eall")
    epart = sb("epart", [P, 32], F32); bepart = Buf("epart")
    sall = [sb(f"sall{i}", [P, 16], F32) for i in range(2)]; bsall = Buf("sall")
    spart = sb("spart", [P, 16], F32); bspart = Buf("spart")
    tmpw = sb("tmpw", [P, 256], F32); btmpw = Buf("tmpw")

    PS = [es.enter_context(nc.psum_tensor(f"ps{i}", [P, 512], F32)) for i in range(8)]
    bPS = [Buf(f"ps{i}") for i in range(8)]

    identf = constsf[:, 0:P]

    def HT3(k, a, b):
        return HT[:, k * 512 + a: k * 512 + b]

    def dma(q, out, in_, r, w, key, nowaw=False, extra=()):
        return S.add(q, lambda e: e.dma_start(out=out, in_=in_), r=r, w=w, dma=key, nowaw=nowaw, extra=extra)

    def act(out, in_, func, r, w, **kw):
        return S.add("act", lambda e: e.activation(out=out, in_=in_, func=func, **kw), r=r, w=w)

    def mm(out, lhsT, rhs, start, stop, r, w):
        return S.add("pe", lambda e: e.matmul(out, lhsT, rhs, start=start, stop=stop), r=r, w=w)

    def tr(out, in_, ident, r, w):
        return S.add("pe", lambda e: e.transpose(out, in_, ident), r=r, w=w)

    def tt(eng, out, in0, in1, op, r, w):
        return S.add(eng, lambda e: e.tensor_tensor(out=out, in0=in0, in1=in1, op=op), r=r, w=w)

    def ts(eng, out, in0, s1, s2, op0, op1, r, w):
        if s2 is None:
            return S.add(eng, lambda e: e.tensor_scalar(out=out, in0=in0, scalar1=s1, scalar2=None, op0=op0), r=r, w=w)
        return S.add(eng, lambda e: e.tensor_scalar(out=out, in0=in0, scalar1=s1, scalar2=s2, op0=op0, op1=op1), r=r, w=w)

    def stt(eng, out, in0, scalar, in1, op0, op1, r, w):
        return S.add(eng, lambda e: e.scalar_tensor_tensor(out=out, in0=in0, scalar=scalar, in1=in1, op0=op0, op1=op1),
                     r=r, w=w)

    def cp(eng, out, in_, r, w):
        return S.add(eng, lambda e: e.tensor_copy(out, in_), r=r, w=w)

    def memset(eng, ap, val, w):
        return S.add(eng, lambda e: e.memset(ap, val), w=w)

    def cast(dst, src, rows, buf, key, piece=256):
        for r0 in range(0, rows, piece):
            r1 = min(rows, r0 + piece)
            dma("pool", dst[r0:r1, :], src[r0:r1, :], [], [buf], key, nowaw=True)

    early_pieces = []
    win_pieces = []
    def cast_list(lst, dst, src, rows, buf, key, piece=256):
        for r0 in range(0, rows, piece):
            r1 = min(rows, r0 + piece)
            lst.append(lambda extra=(), r0=r0, r1=r1: dma("pool", dst[r0:r1, :], src[r0:r1, :], [], [buf], key,
                                                          nowaw=True, extra=extra))
    for j0 in range(0, NG * P, 256):
        for k_ in ("g", "u", "d"):
            cast_list(early_pieces, wffb[(k_, 0)][j0:j0 + 256, :], wff_d[(k_, 0)][j0:j0 + 256, :], 256, B.cast_ff0, "c_ff0")
    cast_list(win_pieces, winb, win_d, 44 * P, B.cast_win, "c_win")
    cast_list(win_pieces, lruwb, lruw_d, P, B.cast_lruw, "c_lruw", piece=32)

    late_pieces = []
    def cast_later(dst, src, rows, buf, key, piece=256):
        for r0 in range(0, rows, piece):
            r1 = min(rows, r0 + piece)
            late_pieces.append(lambda r0=r0, r1=r1: dma("pool", dst[r0:r1, :], src[r0:r1, :], [], [buf], key, nowaw=True))
    cast_later(woutb, wout_d, 8 * P, B.cast_wout, "c_wout")
    for k_ in ("g", "u", "d"):
        cast_later(wffb[(k_, 1)], wff_d[(k_, 1)], NG * P, B.cast_ff1, "c_ff1")

    dma("sp", smalls[:], smalls_d, [], [bsm], "smalls")
    dma("sp", constsf[:], consts_d, [], [bcf], "constsf")
    dma("sp", gkrow[0:1, 0:256], gkrow_d, [], [bgkrow], "stg0")
    cp("dve", identb[:], constsf[:, 0:P], [bcf], [bconst])
    cp("dve", permb[:], constsf[:, P:2 * P], [bcf], [bconst])
    memset("dve", onesb[:], 1.0, [bconst])
    memset("dve", onesf[:], 1.0, [bconst])
    memset("dve", epsc[:, 0:1], EPS, [bconst])
    memset("dve", epsc[:, 1:2], 1.0, [bconst])
    S.add("dve", lambda e: e.tensor_reduce(out=gkm[0:1, 0:1], in_=gkrow[0:1, 0:128], axis=mybir.AxisListType.X,
                                           op=ALU.max, apply_absolute_value=True), r=[bgkrow], w=[bgkm])
    S.add("dve", lambda e: e.tensor_reduce(out=gkm[0:1, 1:2], in_=gkrow[0:1, 128:256], axis=mybir.AxisListType.X,
                                           op=ALU.max, apply_absolute_value=True), r=[bgkrow], w=[bgkm])
    tt("dve", gkm[0:1, 2:3], gkm[0:1, 0:1], gkm[0:1, 1:2], ALU.mult, [bgkm], [bgkm])
    ts("dve", gkm[0:1, 3:4], gkm[0:1, 2:3], -float(np.sqrt(128.0)), None, ALU.mult, None, [bgkm], [bgkm])
    mm(PS[7][:, 0:1], onesf[0:1, :], gkm[0:1, 3:4], True, True, [bgkm, bconst], [bPS[7]])
    cp("dve", nbias[:], PS[7][:, 0:1], [bPS[7]], [bnbias])

    act(scT[:], smalls[:, SM_CV:SM_CV + 32], AF.Silu, [bsm], [bscT])

    class ModStream:
        def __init__(self, chunks, ring, npf, banks):
            self.chunks = chunks; self.ring = ring; self.npf = npf; self.loaded = 0; self.done = 0; self.banks = banks
            self.load_ops = []

        def _load(self):
            i = self.loaded
            c = self.chunks[i]
            tl, bf, key = self.ring[i % len(self.ring)]
            self.load_ops.append(dma("sp", tl, wmod_d[c * P:(c + 1) * P, :], [], [bf], key))
            self.loaded += 1

        def step(self, nsteps=1):
            for _ in range(nsteps):
                if self.done >= len(self.chunks):
                    return
                while self.loaded < min(len(self.chunks), self.done + self.npf + 1):
                    self._load()
                i = self.done
                c = self.chunks[i]
                tl, bf, key = self.ring[i % len(self.ring)]
                (pm, bpm), (ptp, bptp) = self.banks(i)
                st, bst = stg[i % 2], bstg[i % 2]
                for k in range(KC):
                    mm(pm, scT[:, 2 * k:2 * k + 2], tl[:, k * P:(k + 1) * P], k == 0, k == KC - 1, [bscT, bf], [bpm])
                act(st[0:2, 0:P], pm, AF.Copy, [bpm], [bst])
                dma("pool", modraw_d[0:2, c * P:(c + 1) * P], st[0:2, 0:P], [bst], [B.modraw], "modraw", nowaw=True)
                tr(ptp, st[0:2, 0:P], identf[0:2, 0:2], [bst, bcf], [bptp])
                ts("dve", modT[:, 2 * c:2 * c + 2], ptp, smalls[:, SM_BM + c:SM_BM + c + 1], None, ALU.add, None,
                   [bptp, bsm], [bmodT])
                self.done += 1

    def ABs(w_, i, ab):
        o = ((w_ * 3 + i) * 2 + ab) * 16
        return AB[:, o:o + 16]
    def modv(i_mod, w_):
        base = i_mod * 32 + w_
        return modT[:, base:base + 32:2]
    def make_AB(i):
        for w_ in range(2):
            ts("dve", ABs(w_, i, 0), modv(3 * i + 1, w_), 1.0, None, ALU.add, None, [bmodT], [bAB])
            tt("dve", ABs(w_, i, 0), ABs(w_, i, 0), smalls[:, SM_NG + 16 * i:SM_NG + 16 * i + 16], ALU.mult,
               [bAB, bsm], [bAB])
            cp("dve", ABs(w_, i, 1), modv(3 * i, w_), [bmodT], [bAB])

    grows_d = dscr("grows", [4, D], F32); B.grows = Buf("grows")
    GROWS = ((1, 2, 1, 0.5), (0, 2, 1, 0.5), (0, 5, 2, 1.0), (0, 8, 3, 0.5))
    def make_grow(gi_, ta, bta, ka, tb, btb, kb):
        w_, i_mod, rowi, scale = GROWS[gi_]
        dma("sp", ta[0:1, :], modraw_d[w_:w_ + 1, i_mod * D:(i_mod + 1) * D], [B.modraw], [bta], ka)
        dma("sp", tb[0:1, :], rows_d[rowi:rowi + 1, :], [], [btb], kb)
        tt("dve", ta[0:1, :], ta[0:1, :], tb[0:1, :], ALU.add, [bta, btb], [bta])
        if scale != 1.0:
            ts("dve", ta[0:1, :], ta[0:1, :], float(scale), None, ALU.mult, None, [bta], [bta])
        dma("pool", grows_d[gi_:gi_ + 1, :], ta[0:1, :], [bta], [B.grows], "growst")

    ms0 = ModStream(list(range(0, 80)), [(BT[i][:, 0:D], bBT[i], f"bt{i}") for i in range(8)], 6,
                    lambda i: ((PS[i % 2][0:2, 0:P], bPS[i % 2]), (PS[2 + i % 2][:, 0:2], bPS[2 + i % 2])))
    ep_i = 0
    for c_ in range(48):
        ms0.step(1)
        while ep_i < len(early_pieces) and ep_i < (c_ + 1) * 0.7:
            early_pieces[ep_i](extra=(ms0.load_ops[min(len(ms0.load_ops) - 1, c_ + 2)],))
            ep_i += 1
    while ep_i < len(early_pieces):
        early_pieces[ep_i]()
        ep_i += 1
    make_AB(0)
    make_grow(0, XN[0], bXN[0], "xn0", GT[1], bGT[1], "gt1")
    make_grow(1, XN[0], bXN[0], "xn0", GT[1], bGT[1], "gt1")
    ms0.step(32)
    make_AB(1)
    ms1 = ModStream(list(range(80, 144)), [(XN[0][:], bXN[0], "xn0"), (GT[0][:], bGT[0], "gt0")], 1,
                    lambda i: ((PS[7][0:2, 0:P], bPS[7]), (PS[7][:, 256:258], bPS[7])))

    def load_gate(gt_i, gi_):
        dma("sp", GT[gt_i][:], grows_d[gi_:gi_ + 1, :].partition_broadcast(P), [B.grows], [bGT[gt_i]], f"gt{gt_i}")

    plan = []
    def plan_ffn(f, ph):
        idx = []
        for j in range(NG):
            g_ = len(plan); plan.append((wffb[("g", f)][j * P:(j + 1) * P, :], getattr(B, f"cast_ff{f}"), ph))
            u_ = len(plan); plan.append((wffb[("u", f)][j * P:(j + 1) * P, :], getattr(B, f"cast_ff{f}"), ph))
            d_ = len(plan); plan.append((wffb[("d", f)][j * P:(j + 1) * P, :], getattr(B, f"cast_ff{f}"), ph))
            idx.append((g_, u_, d_))
        return idx
    def plan_win(groups, ph):
        idx = {}
        for g_ in groups:
            idx[g_] = len(plan); plan.append((winb[g_ * P:(g_ + 1) * P, :], B.cast_win, ph))
        return idx
    def plan_wout(ph):
        idx = []
        for g_ in range(8):
            idx.append(len(plan)); plan.append((woutb[g_ * P:(g_ + 1) * P, :], B.cast_wout, ph))
        return idx

    tiles1 = [("c", 0, CT // P)] + [("x", i, 4) for i in range(T // TT)]
    plan1a = [plan_ffn(0, 1) for _ in tiles1]
    plan1b = [plan_win(range(8, 20) if kind == "c" else range(44), 1) for (kind, ti, nsub) in tiles1]
    plan3 = []
    for ti in range(T // TT):
        wo = plan_wout(3)
        fi = plan_ffn(1, 3)
        plan3.append((wo, fi))

    wstate = {"next": 0}
    def wprefetch(oldest):
        ph = plan[oldest][2]
        lim = min(len(plan), oldest + NRING)
        while wstate["next"] < lim and plan[wstate["next"]][2] == ph:
            i = wstate["next"]
            src, cb_, _ = plan[i]
            dma("sp", WR[i % NRING][:, 0:4096], src, [cb_], [bWR[i % NRING]], f"wr{i % NRING}")
            wstate["next"] += 1
    def wget(n):
        assert wstate["next"] > n, (wstate, n)
        return WR[n % NRING], bWR[n % NRING]

    evq = {"i": 0}
    bssqc = [Buf(f"ssq{i}") for i in range(8)]
    brstdc = [Buf(f"rstd{i}") for i in range(8)]
    BT4bf = BT[4][:, 0:D].bitcast(BF16)
    BT5bf = BT[5][:, 0:D].bitcast(BF16)
    def HTv(bi, k, a, b):
        if bi == 0:
            return HT[:, k * 512 + a: k * 512 + b]
        t_ = BT4bf if k < 8 else BT5bf
        return t_[:, (k % 8) * 512 + a:(k % 8) * 512 + b]
    def HTb(bi):
        return bHT if bi == 0 else [bBT[4], bBT[5]]

    def norm_part0(src_ap, src_buf, s, xi):
        xn, bxn = XN[xi], bXN[xi]
        act(junk[:], src_ap, AF.Square, [src_buf], [bjunk, bssqc[s]], accum_out=ssq[:, s:s + 1])
        act(rstd[:, s:s + 1], ssq[:, s:s + 1], AF.Sqrt, [bssqc[s], bconst], [brstdc[s]], bias=epsc[:, 0:1], scale=1.0 / D)
        S.add("dve", lambda e: e.reciprocal(out=rstd[:, s:s + 1], in_=rstd[:, s:s + 1]), r=[brstdc[s]], w=[brstdc[s]])
        act(xn[:], src_ap, AF.Copy, [src_buf, brstdc[s]], [bxn], scale=rstd[:, s:s + 1])

    def norm_part1(s, xi, w_, i, hb=0):
        xn, bxn = XN[xi], bXN[xi]
        for kb in range(4):
            pt, bpt = PS[6 + kb % 2], bPS[6 + kb % 2]
            for j in range(4):
                k = kb * 4 + j
                tr(pt[:, j * P:(j + 1) * P], xn[:, k * P:(k + 1) * P], identf, [bxn, bcf], [bpt])
            for j in range(4):
                k = kb * 4 + j
                A_ = ABs(w_, i, 0)[:, k:k + 1]
                B_ = ABs(w_, i, 1)[:, k:k + 1]
                o = HTv(hb, k, s * P, (s + 1) * P)
                src = pt[:, j * P:(j + 1) * P]
                if evq["i"] % 3 != 2:
                    ts("dve", o, src, A_, B_, ALU.mult, ALU.add, [bpt, bAB], HTb(hb))
                else:
                    act(o, src, AF.Identity, [bpt, bAB], HTb(hb), scale=A_, bias=B_)
                evq["i"] += 1

    def norm_to_hT(src_tiles, src_bufs, nsub, w_, i, xn_list, hb=0):
        for s in range(nsub):
            xi = xn_list[s % len(xn_list)]
            norm_part0(src_tiles[s], src_bufs[s], s, xi)
            norm_part1(s, xi, w_, i, hb)

    def ffn(fidx, nsub, acc_tiles, acc_bufs, hook=None):
        ntok = nsub * P
        cnt = 0
        for j in range(NG):
            gi, ui, di = fidx[j]
            if hook is not None:
                hook(j)
            wprefetch(gi)
            wg_, bwg = wget(gi)
            wu_, bwu = wget(ui)
            wd_, bwd = wget(di)
            hd, bhd = hid[j % 2], bhid[j % 2]
            for fc in range(GF):
                pg, bpg = PS[2 * (fc % 2)], bPS[2 * (fc % 2)]
                pu, bpu = PS[2 * (fc % 2) + 1], bPS[2 * (fc % 2) + 1]
                for k in range(KC):
                    mm(pg[:, 0:ntok], wg_[:, k * 256 + fc * P:k * 256 + (fc + 1) * P], HT3(k, 0, ntok),
                       k == 0, k == KC - 1, [bwg] + bHT, [bpg])
                for k in range(KC):
                    mm(pu[:, 0:ntok], wu_[:, k * 256 + fc * P:k * 256 + (fc + 1) * P], HT3(k, 0, ntok),
                       k == 0, k == KC - 1, [bwu] + bHT, [bpu])
                sl, bsl = sil[fc % 2], bsil[fc % 2]
                act(sl[:, 0:ntok], pg[:, 0:ntok], AF.Silu, [bpg], [bsl])
                tt("dve", hd[:, fc * 512:fc * 512 + ntok], sl[:, 0:ntok], pu[:, 0:ntok], ALU.mult, [bsl, bpu], [bhd])
            wprefetch(di)
            for s in range(nsub):
                for dg in range(4):
                    pd, bpd = PS[4 + cnt % 4], bPS[4 + cnt % 4]
                    cnt += 1
                    for fc in range(GF):
                        mm(pd[:, :], hd[:, fc * 512 + s * P:fc * 512 + (s + 1) * P],
                           wd_[:, fc * D + dg * 512:fc * D + (dg + 1) * 512], fc == 0, fc == GF - 1,
                           [bhd, bwd], [bpd])
                    a_ = acc_tiles[s][:, dg * 512:(dg + 1) * 512]
                    if j == 0:
                        cp("dve", a_, pd[:, :], [bpd], [acc_bufs[s]])
                    else:
                        tt("dve", a_, a_, pd[:, :], ALU.add, [bpd, acc_bufs[s]], [acc_bufs[s]])

    xt = [BT[s][:, 0:D] for s in range(4)]
    bxt = [bBT[s] for s in range(4)]
    acc = [BT[4 + s][:, 0:D] for s in range(4)]
    bacc = [bBT[4 + s] for s in range(4)]
    stq = {"f": 0, "b": 0, "v": 0}

    def evac_store(pt_ap, bpt, ntok, func, dst_ap, dst_buf, bf):
        if bf:
            i = stq["b"] % 2; stq["b"] += 1
            st_, bst_ = stgb[i], bstgb[i]; key = f"stgb{i}"
        else:
            i = stq["f"] % 2; stq["f"] += 1
            st_, bst_ = stg[i], bstg[i]; key = f"stg{i}"
        act(st_[:, 0:ntok], pt_ap, func, [bpt], [bst_])
        dma("pool", dst_ap, st_[:, 0:ntok], [bst_], [dst_buf], key, nowaw=True)
        return st_, bst_

    sqb2 = [sqb, sb("sqb1", [P, TT], BF16)]; bsqb2 = [bsqb, Buf("sqb1")]
    rsd2 = [rsd, sb("rsd1", [P, TT], F32)]; brsd2 = [brsd, Buf("rsd1")]
    qnb2 = [qnb, sb("qnb1", [P, TT], BF16)]; bqnb2 = [bqnb, Buf("qnb1")]

    class QKPipe:
        def __init__(self):
            self.items = []
            self.s2 = 0
            self.s3 = 0
            self.fresh = False

        def push(self, pt, bpt, ntok, gidx, rope_on, dst_ap, dst_buf):
            c = len(self.items)
            self.items.append((pt, bpt, ntok, gidx, rope_on, dst_ap, dst_buf))
            self.fresh = True
            act(sqb2[c % 2][:, 0:ntok], pt[:, 0:ntok], AF.Square, [bpt], [bsqb2[c % 2]])

        def advance(self):
            if self.s3 < self.s2:
                self._stage3(self.s3)
                self.s3 += 1
            lim = len(self.items) - (1 if self.fresh else 0)
            if self.s2 < lim:
                self._stage2(self.s2)
                self.s2 += 1
            self.fresh = False

        def _stage2(self, c):
            pt, bpt, ntok, gidx, rope_on, dst_ap, dst_buf = self.items[c]
            p2, bp2 = PS[4 + c % 2], bPS[4 + c % 2]
            mm(p2[:, 0:ntok], onesb[:], sqb2[c % 2][:, 0:ntok], True, True, [bconst, bsqb2[c % 2]], [bp2])
            r_, br_ = rsd2[c % 2], brsd2[c % 2]
            act(r_[:, 0:ntok], p2[:, 0:ntok], AF.Ln, [bp2, bconst], [br_], bias=epsc[:, 0:1], scale=1.0 / 128.0)
            act(r_[:, 0:ntok], r_[:, 0:ntok], AF.Exp, [br_], [br_], scale=-0.5)
            gcol = smalls[:, SM_QG + gidx:SM_QG + gidx + 1]
            if rope_on:
                stt("dve", qnb2[c % 2][:, 0:ntok], pt[:, 0:ntok], gcol, r_[:, 0:ntok], ALU.mult, ALU.mult,
                    [bpt, bsm, br_], [bqnb2[c % 2]])
            else:
                i = stq["b"] % 2; stq["b"] += 1
                st_, bst_ = stgb[i], bstgb[i]
                stt("dve", st_[:, 0:ntok], pt[:, 0:ntok], gcol, r_[:, 0:ntok], ALU.mult, ALU.mult, [bpt, bsm, br_], [bst_])
                dma("pool", dst_ap, st_[:, 0:ntok], [bst_], [dst_buf], f"stgb{i}", nowaw=True)

        def _stage3(self, c):
            pt, bpt, ntok, gidx, rope_on, dst_ap, dst_buf = self.items[c]
            if not rope_on:
                return
            q_, bq_ = qnb2[c % 2], bqnb2[c % 2]
            p3, bp3 = PS[6 + c % 2], bPS[6 + c % 2]
            mm(p3[:, 0:ntok], permb[:], q_[:, 0:ntok], True, True, [bconst, bq_], [bp3])
            tt("pool", t1[:, 0:ntok], q_[:, 0:ntok], ropet[:, 0:ntok], ALU.mult, [bq_, bropet], [bt1])
            tt("dve", t2[:, 0:ntok], p3[:, 0:ntok], ropet[:, TT:TT + ntok], ALU.mult, [bp3, bropet], [bt2])
            i = stq["b"] % 2; stq["b"] += 1
            st_, bst_ = stgb[i], bstgb[i]
            tt("dve", st_[:, 0:ntok], t1[:, 0:ntok], t2[:, 0:ntok], ALU.add, [bt1, bt2], [bst_])
            dma("pool", dst_ap, st_[:, 0:ntok], [bst_], [dst_buf], f"stgb{i}", nowaw=True)

        def flush(self):
            self.fresh = False
            while self.s3 < len(self.items):
                self.advance()

    x1c_d = dscr("x1c", [CT, D], F32); B.x1c = Buf("x1c")

    for tix, (kind, ti, nsub) in enumerate(tiles1):
        ntok = nsub * P
        w_ = 1 if kind == "c" else 0
        fidx = plan1a[tix]
        src_d = ctx_d if kind == "c" else x_d
        r0 = 0 if kind == "c" else ti * TT
        if tix == 0:
            load_gate(0, 0)
        if tix == 1:
            load_gate(0, 1)
        for s in range(nsub):
            dma("sp", xt[s], src_d[r0 + s * P:r0 + (s + 1) * P, :], [], [bxt[s]], f"bt{s}")
        norm_to_hT(xt, bxt, nsub, w_, 0, [0, 1])
        def hook1a(j, tix=tix):
            if tix >= 1 and j % 3 == 0 and win_pieces:
                win_pieces.pop(0)()
        ffn(fidx, nsub, acc, bacc, hook=hook1a)
        if tix == len(tiles1) - 1:
            while win_pieces:
                win_pieces.pop(0)()
        for s in range(nsub):
            tt("dve", acc[s], acc[s], GT[0][:], ALU.mult, [bacc[s], bGT[0]], [bacc[s]])
            tt("dve", acc[s], acc[s], xt[s], ALU.add, [bxt[s], bacc[s]], [bacc[s]])
            if kind == "x":
                dma("pool", x1_d[r0 + s * P:r0 + (s + 1) * P, :], acc[s], [bacc[s]], [B.x1], f"x1st{s}", nowaw=True)
            else:
                dma("pool", x1c_d[s * P:(s + 1) * P, :], acc[s], [bacc[s]], [B.x1c], f"x1st{s}", nowaw=True)

    def load_x1_tile(tix):
        kind, ti, nsub = tiles1[tix]
        r0 = 0 if kind == "c" else ti * TT
        for s in range(nsub):
            if kind == "x":
                dma("sp", xt[s], x1_d[r0 + s * P:r0 + (s + 1) * P, :], [B.x1], [bxt[s]], f"bt{s}")
            else:
                dma("sp", xt[s], x1c_d[s * P:(s + 1) * P, :], [B.x1c], [bxt[s]], f"bt{s}")

    load_x1_tile(0)
    norm_to_hT(xt, bxt, tiles1[0][2], 1 if tiles1[0][0] == "c" else 0, 1, [0, 1], hb=0)
    for tix, (kind, ti, nsub) in enumerate(tiles1):
        ntok = nsub * P
        hb = tix % 2
        widx = plan1b[tix]
        r0 = 0 if kind == "c" else ti * TT
        if kind == "x":
            dma("sp", ropet[:, 0:TT], rope_d[:, r0:r0 + TT], [], [bropet], "ropet", nowaw=True)
            dma("sp", ropet[:, TT:2 * TT], rope_d[:, T + r0:T + r0 + TT], [], [bropet], "ropet", nowaw=True)
        pieces = []
        if tix + 1 < len(tiles1):
            kind2, ti2, nsub2 = tiles1[tix + 1]
            w2 = 1 if kind2 == "c" else 0
            pieces.append(lambda: load_x1_tile(tix + 1))
            for s2 in range(nsub2):
                pieces.append(lambda s2=s2: norm_part0(xt[s2], bxt[s2], s2, s2 % 2))
                pieces.append(lambda s2=s2, w2=w2, hb=hb: norm_part1(s2, s2 % 2, w2, 1, 1 - hb))
        nticks = sum((nsub if g_ in (10, 11) else 2) for g_ in widx)
        start = 30 if kind == "x" else 5
        step = max(1, (nticks - start - 2) // max(1, len(pieces)))
        tk = {"i": 0}
        def tick():
            tk["i"] += 1
            if pieces and tk["i"] >= start and (tk["i"] - start) % step == 0:
                pieces.pop(0)()
        pcount = 0
        qk = QKPipe()
        for g_ in sorted(widx.keys()):
            wprefetch(widx[g_])
            ws, bws = wget(widx[g_])
            if g_ in (10, 11):
                for s in range(nsub):
                    pv, bpv = PS[pcount % 4], bPS[pcount % 4]; pcount += 1
                    for k in range(KC):
                        mm(pv[:, 0:256], HTv(hb, k, s * P, (s + 1) * P), ws[:, k * 256:(k + 1) * 256], k == 0, k == KC - 1,
                           [bws] + HTb(hb), [bpv])
                    qk.advance()
                    i = stq["v"] % 2; stq["v"] += 1
                    cp("dve", vst[i][:], pv[:, 0:256], [bpv], [bvst[i]])
                    c0 = (g_ - 10) * 256
                    if kind == "c":
                        dma("pool", vc_d[s * P:(s + 1) * P, c0:c0 + 256], vst[i][:], [bvst[i]], [B.vc], f"vst{i}", nowaw=True)
                    else:
                        dma("pool", vo_d[r0 + s * P:r0 + (s + 1) * P, c0:c0 + 256], vst[i][:], [bvst[i]], [B.vo],
                            f"vst{i}", nowaw=True)
                    tick()
                continue
            for ci in range(2):
                cc = 2 * g_ + ci
                pt, bpt = PS[pcount % 4], bPS[pcount % 4]; pcount += 1
                for k in range(KC):
                    mm(pt[:, 0:ntok], ws[:, k * 256 + ci * P:k * 256 + (ci + 1) * P], HTv(hb, k, 0, ntok), k == 0, k == KC - 1,
                       [bws] + HTb(hb), [bpt])
                if cc < 16:
                    qk.push(pt, bpt, ntok, 0, True, qT_d[cc * P:(cc + 1) * P, r0:r0 + ntok], B.qT)
                elif cc < 20:
                    kv = cc - 16
                    if kind == "c":
                        qk.push(pt, bpt, ntok, 1, False, kTc_d[:, kv * CT:(kv + 1) * CT], B.kTc)
                    else:
                        qk.push(pt, bpt, ntok, 1, True, kTo_d[:, kv * T + r0:kv * T + r0 + ntok], B.kTo)
                qk.advance()
                if cc < 20:
                    pass
                elif cc < 40:
                    n = cc - 24
                    if kind == "c":
                        evac_store(pt[:, 0:ntok], bpt, ntok, AF.Copy, lxc_d[n * P:(n + 1) * P, 0:CT], B.lxc, False)
                    else:
                        st_, bst_ = evac_store(pt[:, 0:ntok], bpt, ntok, AF.Copy, lx_d[n * P:(n + 1) * P, r0:r0 + ntok],
                                               B.lx, False)
                        if ti == T // TT - 1:
                            cp("dve", edgt[:, 2 * n:2 * n + 1], st_[:, ntok - 1:ntok], [bst_], [bedgt])
                            cp("dve", edgt[:, 2 * n + 1:2 * n + 2], st_[:, ntok - 2:ntok - 1], [bst_], [bedgt])
                elif cc < 56:
                    n = cc - 40
                    evac_store(pt[:, 0:ntok], bpt, ntok, AF.Gelu, lg_d[n * P:(n + 1) * P, r0:r0 + ntok], B.lg, True)
                elif cc < 72:
                    n = cc - 56
                    evac_store(pt[:, 0:ntok], bpt, ntok, AF.Sigmoid, ga_d[n * P:(n + 1) * P, r0:r0 + ntok], B.ga, True)
                else:
                    n = cc - 72
                    evac_store(pt[:, 0:ntok], bpt, ntok, AF.Gelu if False else AF.Sigmoid, gl_d[n * P:(n + 1) * P, r0:r0 + ntok], B.gl, True)
                tick()
        qk.flush()
        while pieces:
            pieces.pop(0)()

    dma("pool", edg_d, edgt[:], [bedgt], [B.edg], "edgst")
    rg = [[2 * i, 2 * i + 1] for i in range(n_cores // 2)]
    def coll(ins_, outs_, r, w, key):
        S.add("pool", lambda e: e.collective_compute("AllGather", ALU.bypass, replica_groups=rg, ins=[ins_], outs=[outs_]),
              r=r, w=w, dma=key, inc=1)
    coll(kTo_d, kTa_d, [B.kTo], [B.kTa], "cc_k")
    coll(vo_d, va_d, [B.vo], [B.va], "cc_v")
    coll(edg_d, edga_d, [B.edg], [B.edga], "cc_e")
    dma("sp", eall[0][:], edga_d[0:P, :], [B.edga], [beall], "eall", nowaw=True)
    dma("sp", eall[1][:], edga_d[P:2 * P, :], [B.edga], [beall], "eall", nowaw=True)
    ts("dve", epart[:], eall[0][:], smalls[:, SM_FB:SM_FB + 1], None, ALU.mult, None, [beall, bsm], [bepart])
    stt("dve", epart[:], eall[1][:], smalls[:, SM_FA:SM_FA + 1], epart[:], ALU.mult, ALU.add, [beall, bsm, bepart], [bepart])

    lwq = {"i": 0}
    act(clt[:, 0:32], smalls[:, SM_LAM:SM_LAM + 32], AF.Exp, [bsm], [bclt], scale=-1.0)
    act(clt[:, 0:32], clt[:, 0:32], AF.Ln, [bclt, bconst], [bclt], bias=epsc[:, 1:2], scale=1.0)
    ts("dve", clt[:, 32:64], clt[:, 0:32], -8.0, None, ALU.mult, None, [bclt], [bclt])
    ts("dve", clt[:, 64:96], clt[:, 0:32], -16.0, None, ALU.mult, None, [bclt], [bclt])

    gbank = {"i": 0}
    def lru_load_w(n, dr, li):
        ma, mx = 2 * dr, 2 * dr + 1
        lruW, blruW = lruWt[li], blruWt[li]
        dma("sp", lruW[:, 0:P], lruwb[:, (ma * 16 + n) * P:(ma * 16 + n + 1) * P], [B.cast_lruw], [blruW], f"lruW{li}", nowaw=True)
        dma("sp", lruW[:, P:2 * P], lruwb[:, (mx * 16 + n) * P:(mx * 16 + n + 1) * P], [B.cast_lruw], [blruW], f"lruW{li}", nowaw=True)

    def lru_h2(n, dr, xc_ap, xc16_ap, bxc, bxc16, ntok, ra, bra, ix, bix, a2, ba2, li):
        ma, mx = 2 * dr, 2 * dr + 1
        lruW, blruW = lruWt[li], blruWt[li]
        for (m_, dst, bdst) in ((ma, ra, bra), (mx, ix, bix)):
            for t4 in range(0, ntok, 512):
                w4 = min(512, ntok - t4)
                gi_ = 6 + gbank["i"] % 2; gbank["i"] += 1
                pg, bpg = PS[gi_], bPS[gi_]
                mm(pg[:, 0:w4], lruW[:, (m_ % 2) * P:(m_ % 2 + 1) * P], xc16_ap[:, t4:t4 + w4], True, True,
                   [blruW, bxc16], [bpg])
                act(dst[:, t4:t4 + w4], pg[:, 0:w4], AF.Sigmoid, [bpg, bsm], [bdst],
                    bias=smalls[:, SM_LB + m_ * 16 + n:SM_LB + m_ * 16 + n + 1], scale=1.0)
        cl = clt[:, 32 + dr * 16 + n:32 + dr * 16 + n + 1]
        act(ra[:, 0:ntok], ra[:, 0:ntok], AF.Exp, [bra, bclt], [bra], scale=cl)
        tt("pool", a2[:, 0:ntok], ra[:, 0:ntok], ra[:, 0:ntok], ALU.mult, [bra], [ba2])
        ts("dve", a2[:, 0:ntok], a2[:, 0:ntok], -1.0, 1.0, ALU.mult, ALU.add, [ba2], [ba2])
        tt("dve", ix[:, 0:ntok], ix[:, 0:ntok], xc_ap[:, 0:ntok], ALU.mult, [bix, bxc], [bix])

    def lru_h3(ntok, ra, bra, ix, bix, a2, ba2, h_ap, bh, init_ap, binit, rev):
        act(a2[:, 0:ntok], a2[:, 0:ntok], AF.Sqrt, [ba2], [ba2])
        tt("dve", ix[:, 0:ntok], ix[:, 0:ntok], a2[:, 0:ntok], ALU.mult, [bix, ba2], [bix])
        if rev:
            S.add("dve", lambda e: e.tensor_tensor_scan(out=h_ap[:, 0:ntok][:, ::-1],
                                                        data0=ra[:, 0:ntok][:, ::-1], data1=ix[:, 0:ntok][:, ::-1],
                                                        initial=init_ap, op0=ALU.mult, op1=ALU.add),
                  r=[bra, bix, binit], w=[bh])
        else:
            S.add("dve", lambda e: e.tensor_tensor_scan(out=h_ap[:, 0:ntok], data0=ra[:, 0:ntok], data1=ix[:, 0:ntok],
                                                        initial=init_ap, op0=ALU.mult, op1=ALU.add),
                  r=[bra, bix, binit], w=[bh])

    def conv5(n, src, bsrc, ntok, xc, bxc):
        cw = lambda j: smalls[:, SM_CW + n * 5 + j:SM_CW + n * 5 + j + 1]
        ts("dve", xc[:, 0:ntok], src[:, 0:ntok], cw(0), smalls[:, SM_CB + n:SM_CB + n + 1], ALU.mult, ALU.add,
           [bsrc, bsm], [bxc])
        for j in range(1, 5):
            stt("dve", xc[:, 0:ntok], src[:, j:j + ntok], cw(j), xc[:, 0:ntok], ALU.mult, ALU.add, [bsrc, bsm, bxc], [bxc])

    LXP, bLXP = BT[0], bBT[0]
    XC, bXC = BT[1], bBT[1]
    RA, bRA = BT[2], bBT[2]
    IX, bIX = BT[3], bBT[3]
    A2, bA2 = BT[4], bBT[4]
    HH, bHH = BT[5], bBT[5]
    ATT, bATT = BT[6], bBT[6]
    HFL, bHFL = BT[7], bBT[7]
    XC16 = HT[:, 0:2048]; bXC16 = bHT[0]
    LGt = HT[:, 2048:4096]; bLGt = bHT[1]
    MGo = WR[4][:, 0:T]; bMGo = bWR[4]
    GAt = HT[:, 4096:6144]; bGAt = bHT[2]
    GLt = HT[:, 6144:8192]; bGLt = bHT[3]

    memset("dve", lxc[:, 0:2], 0.0, [blxc])
    memset("dve", lxc[:, CT + 2:CT + 4], 0.0, [blxc])
    memset("dve", LXP[:, 0:2], 0.0, [bLXP])
    memset("dve", hce[:], 0.0, [bhce])
    zero_col = sb("zero_col", [P, 1], F32); bzc = Buf("zc")
    memset("dve", zero_col[:], 0.0, [bzc])

    at_d = dscr("att", [KC * P, T], F32); B.att = Buf("att")
    ATL, bATL = GT[1], bGT[1]

    def lx_load_conv(n):
        dma("sp", LXP[:, 2:2 + T], lx_d[n * P:(n + 1) * P, :], [B.lx], [bLXP], "bt0")
        cp("dve", LXP[:, 2 + T:4 + T], epart[:, 2 * n:2 * n + 2], [bepart], [bLXP])
        conv5(n, LXP, bLXP, T, XC, bXC)
        cp("pool", XC16, XC[:, 0:T], [bXC], [bXC16])

    def s1_h1(n):
        lru_load_w(n, 0, n % 2)
        dma("sp", lxc[:, 2:2 + CT], lxc_d[n * P:(n + 1) * P, :], [B.lxc], [blxc], "lxct")
        conv5(n, lxc, blxc, CT, cxt[0], bcxt[0])
        cp("pool", cx16[:], cxt[0][:], [bcxt[0]], [bcx16])
        lx_load_conv(n)

    def s1_h2(n):
        lru_h2(n, 0, cxt[0], cx16, bcxt[0], bcx16, CT, cxt[1], bcxt[1], cxt[2], bcxt[2], cxt[3], bcxt[3], n % 2)
        lru_h2(n, 0, XC, XC16, bXC, bXC16, T, RA, bRA, IX, bIX, A2, bA2, n % 2)

    def s1_h3(n):
        lru_h3(CT, cxt[1], bcxt[1], cxt[2], bcxt[2], cxt[3], bcxt[3], cxt[4], bcxt[4], zero_col[:, 0:1], bzc, False)
        cp("dve", hce[:, n:n + 1], cxt[4][:, CT - 1:CT], [bcxt[4]], [bhce])
        lru_h3(T, RA, bRA, IX, bIX, A2, bA2, HH, bHH, hce[:, n:n + 1], bhce, False)
        cp("dve", estt[:, n:n + 1], HH[:, T - 1:T], [bHH], [bestt])
        dma("pool", hf_d[n * P:(n + 1) * P, :], HH[:, 0:T], [bHH], [B.hf], "hfst", nowaw=True)

    def exchange_states():
        dma("pool", est_d, estt[:], [bestt], [B.est], "estst")
        coll(est_d, esta_d, [B.est], [B.esta], "cc_s")
        dma("sp", sall[0][:], esta_d[0:P, :], [B.esta], [bsall], "sall", nowaw=True)
        dma("sp", sall[1][:], esta_d[P:2 * P, :], [B.esta], [bsall], "sall", nowaw=True)
        ts("dve", spart[:], sall[0][:], smalls[:, SM_FB:SM_FB + 1], None, ALU.mult, None, [bsall, bsm], [bspart])
        stt("dve", spart[:], sall[1][:], smalls[:, SM_FA:SM_FA + 1], spart[:], ALU.mult, ALU.add,
            [bsall, bsm, bspart], [bspart])

    def m_h2(n):
        lru_h2(n, 1, XC, XC16, bXC, bXC16, T, RA, bRA, IX, bIX, A2, bA2, (n + 1) % 2)
        dma("sp", HFL[:, 0:T], hf_d[n * P:(n + 1) * P, :], [B.hf], [bHFL], "bt7")
        dma("sp", ATL[:], at_d[n * P:(n + 1) * P, :], [B.att], [bATL], "gt1")
        dma("sp", LGt, lg_d[n * P:(n + 1) * P, :], [B.lg], [bLGt], "ht1")
        dma("sp", GAt, ga_d[n * P:(n + 1) * P, :], [B.ga], [bGAt], "ht2")
        dma("sp", GLt, gl_d[n * P:(n + 1) * P, :], [B.gl], [bGLt], "ht3")

    def m_h3(n):
        lru_h3(T, RA, bRA, IX, bIX, A2, bA2, HH, bHH, spart[:, n:n + 1], bspart, True)
        tt("dve", HH[:, 0:T], HH[:, 0:T], HFL[:, 0:T], ALU.add, [bHH, bHFL], [bHH])
        tt("pool", HH[:, 0:T], HH[:, 0:T], LGt, ALU.mult, [bHH, bLGt], [bHH])
        tt("pool", HH[:, 0:T], HH[:, 0:T], GLt, ALU.mult, [bHH, bGLt], [bHH])
        tt("dve", ATL[:], ATL[:], GAt, ALU.mult, [bATL, bGAt], [bATL])
        tt("dve", MGo, ATL[:], HH[:, 0:T], ALU.add, [bATL, bHH], [bMGo])
        dma("pool", mg_d[n * P:(n + 1) * P, :], MGo, [bMGo], [B.mg], "mgst", nowaw=True)

    def mod_finish():
        make_AB(2)
        make_grow(2, XN[0], bXN[0], "xn0", GT[0], bGT[0], "gt0")
        make_grow(3, XN[0], bXN[0], "xn0", GT[0], bGT[0], "gt0")

    NBLK = NQ * (T // TT)
    jobs = []
    for n in range(KC):
        jobs.append((lambda n=n: s1_h1(n), lambda n=n: s1_h2(n), lambda n=n: s1_h3(n)))
    jobs.append((None, None, exchange_states))
    for n in range(KC):
        jobs.append((lambda n=n: (lru_load_w(n, 1, (n + 1) % 2), lx_load_conv(n)), lambda n=n: m_h2(n), lambda n=n: m_h3(n)))
    def blk_of(t):
        return min(NBLK, t if t < KC + 2 else max(t, 4 * (t - (KC + 2)) + 4))
    sched2 = {i: [] for i in range(NBLK + 1)}
    for t in range(len(jobs) + 2):
        def slot(t=t):
            for st_i, j in ((2, t - 2), (1, t - 1), (0, t)):
                if 0 <= j < len(jobs) and jobs[j][st_i] is not None:
                    jobs[j][st_i]()
        sched2[blk_of(t)].append(slot)
    for b_ in range(NBLK):
        sched2[b_].append(lambda: ms1.step(1))
        if b_ < len(late_pieces):
            sched2[b_].append(late_pieces[b_])
    assert len(late_pieces) <= NBLK
    sched2[NBLK].append(mod_finish)

    def qload(blk_):
        n_, q4_ = blk_ // (T // TT), blk_ % (T // TT)
        qi_ = blk_ % 2
        dma("sp", qTt[qi_][:], qT_d[n_ * P:(n_ + 1) * P, q4_ * TT:(q4_ + 1) * TT], [B.qT], [bqTt[qi_]], f"qTt{qi_}")

    KT = [WR[0], WR[1]]; bKT = [bWR[0], bWR[1]]
    VT = [WR[2], WR[3]]; bVT = [bWR[2], bWR[3]]
    for n in range(NQ):
        kv = n // 4
        kt_, bkt_ = KT[kv % 2], bKT[kv % 2]
        vt_, bvt_ = VT[kv % 2], bVT[kv % 2]
        if n % 4 == 0:
            key = f"wr{kv % 2}"
            dma("sp", kt_[:, 0:CT], kTc_d[:, kv * CT:(kv + 1) * CT], [B.kTc], [bkt_], key, nowaw=True)
            dma("sp", kt_[:, CT:CT + T], kTa_d[0:P, kv * T:(kv + 1) * T], [B.kTa], [bkt_], key, nowaw=True)
            dma("sp", kt_[:, CT + T:CT + 2 * T], kTa_d[P:2 * P, kv * T:(kv + 1) * T], [B.kTa], [bkt_], key, nowaw=True)
            key = f"wr{2 + kv % 2}"
            for t_ in range(NKT):
                if t_ < 2:
                    src = vc_d[t_ * P:(t_ + 1) * P, kv * P:(kv + 1) * P]; bsrc = B.vc
                else:
                    src = va_d[(t_ - 2) * P:(t_ - 1) * P, kv * P:(kv + 1) * P]; bsrc = B.va
                dma("sp", vt_[:, t_ * P:(t_ + 1) * P], src, [bsrc], [bvt_], key, nowaw=True)
        for q4 in range(T // TT):
            po, bpo = PS[3 + q4 % 2], bPS[3 + q4 % 2]
            pd_, bpd_ = PS[5], bPS[5]
            blk_ = n * 4 + q4
            if blk_ == 0:
                qload(0)
            if blk_ + 1 < NQ * (T // TT):
                qload(blk_ + 1)
            qi_ = blk_ % 2
            qt_, bqt_ = qTt[qi_], bqTt[qi_]
            qs = qt_[:]

            def s_mm(t_):
                ps_, bps_ = PS[t_ % 3], bPS[t_ % 3]
                mm(ps_[:, :], kt_[:, t_ * P:(t_ + 1) * P], qs, True, True, [bkt_, bqt_], [bps_])

            def pv_mm(t_):
                ps_, bps_ = PS[t_ % 3], bPS[t_ % 3]
                pt_, bpt_ = pT[t_ % 3], bpT[t_ % 3]
                act(pt_[:], ps_[:, :], AF.Exp, [bps_, bnbias], [bpt_], bias=nbias[:, 0:1], scale=SM_SCALE)
                mm(po[:, :], vt_[:, t_ * P:(t_ + 1) * P], pt_[:], t_ == 0, t_ == NKT - 1, [bvt_, bpt_], [bpo])
                mm(pd_[:, :], onesb[:], pt_[:], t_ == 0, t_ == NKT - 1, [bconst, bpt_], [bpd_])

            s_mm(0)
            s_mm(1)
            for t_ in range(NKT):
                if t_ + 2 < NKT:
                    s_mm(t_ + 2)
                pv_mm(t_)
            cp("dve", rec[:], pd_[:, :], [bpd_], [brec])
            S.add("dve", lambda e: e.reciprocal(out=rec[:], in_=rec[:]), r=[brec], w=[brec])
            tt("dve", ATT[:, q4 * TT:(q4 + 1) * TT], po[:, :], rec[:], ALU.mult, [bpo, brec], [bATT])
            if q4 == T // TT - 1:
                dma("pool", at_d[n * P:(n + 1) * P, :], ATT[:, 0:T], [bATT], [B.att], "attst", nowaw=True)
            for task in sched2[n * 4 + q4]:
                task()
    for task in sched2[NBLK]:
        task()

    load_gate(1, 3)
    load_gate(0, 2)
    FG, bFG = XN[1], bXN[1]
    dma("sp", FG[:], rows_d[0:1, :].partition_broadcast(P), [], [bFG], "xn1")
    for ti in range(T // TT):
        r0 = ti * TT
        wo_idx, fidx = plan3[ti]
        for s in range(4):
            dma("sp", xt[s], x1_d[r0 + s * P:r0 + (s + 1) * P, :], [B.x1], [bxt[s]], f"bt{s}")
        for k in range(KC):
            dma("sp", HT3(k, 0, TT), mg_d[k * P:(k + 1) * P, r0:r0 + TT], [B.mg], bHT, "htld", nowaw=True)
        cnt = 0
        for g_ in range(8):
            wprefetch(wo_idx[g_])
            ws, bws = wget(wo_idx[g_])
            for s in range(4):
                pw, bpw = PS[4 + cnt % 2], bPS[4 + cnt % 2]; cnt += 1
                for k in range(KC):
                    mm(pw[:, 0:256], HT3(k, s * P, (s + 1) * P), ws[:, k * 256:(k + 1) * 256], k == 0, k == KC - 1,
                       [bws] + bHT, [bpw])
                tt("dve", tmpw[:], pw[:, 0:256], GT[0][:, g_ * 256:(g_ + 1) * 256], ALU.mult, [bpw, bGT[0]], [btmpw])
                tt("dve", xt[s][:, g_ * 256:(g_ + 1) * 256], xt[s][:, g_ * 256:(g_ + 1) * 256], tmpw[:], ALU.add,
                   [bxt[s], btmpw], [bxt[s]])
        norm_to_hT(xt, bxt, 4, 0, 2, [0])
        ffn(fidx, 4, acc, bacc)
        for s in range(4):
            tt("dve", acc[s], acc[s], GT[1][:], ALU.mult, [bacc[s], bGT[1]], [bacc[s]])
            tt("dve", acc[s], acc[s], xt[s], ALU.add, [bxt[s], bacc[s]], [bacc[s]])
            act(junk[:], acc[s], AF.Square, [bacc[s]], [bjunk, bssqc[4 + s]], accum_out=ssq[:, 4 + s:5 + s])
            act(rstd[:, 4 + s:5 + s], ssq[:, 4 + s:5 + s], AF.Sqrt, [bssqc[4 + s], bconst], [brstdc[4 + s]], bias=epsc[:, 0:1],
                scale=1.0 / D)
            S.add("dve", lambda e, s=s: e.reciprocal(out=rstd[:, 4 + s:5 + s], in_=rstd[:, 4 + s:5 + s]),
                  r=[brstdc[4 + s]], w=[brstdc[4 + s]])
            stt("dve", acc[s], acc[s], rstd[:, 4 + s:5 + s], FG[:], ALU.mult, ALU.mult, [bacc[s], brstdc[4 + s], bFG], [bacc[s]])
            dma("pool", out_d[r0 + s * P:r0 + (s + 1) * P, :], acc[s], [bacc[s]], [B.out], f"ost{s}", nowaw=True)

    fin_reads = [B.out]
    if debug:
        for nm, (dst, src) in dbg_out.items():
            bsrc = getattr(B, nm[2:])
            bd = Buf(nm)
            dma("pool", dst, src, [bsrc], [bd], "dbg", nowaw=True)
            fin_reads.append(bd)
    S.add("pool", lambda e: e.nop(), r=fin_reads, w=[])

    S.finalize()
    esem = {e: es.enter_context(nc.semaphore(f"s_{e}")) for e in Sched.ENG}
    dsem = {k: es.enter_context(nc.semaphore(f"d_{k}")) for k in S.dma_count}
    block = es.enter_context(nc.Block())

    @block.tensor
    def _(e):
        S.emit("pe", e, esem, dsem)

    @block.scalar
    def _(e):
        S.emit("act", e, esem, dsem)

    @block.vector
    def _(e):
        S.emit("dve", e, esem, dsem)

    @block.gpsimd
    def _(e):
        S.emit("pool", e, esem, dsem)

    @block.sync
    def _(e):
        S.emit("sp", e, esem, dsem)

    es.close()
    return nc


def _rope_tables(pos):
    f32 = np.float32
    row = (pos // 64).astype(f32)
    col = (pos % 64).astype(f32)
    freqs = (f32(10000.0) ** (-(np.arange(0, 64, 2, dtype=f32)) / f32(64))).astype(f32)
    ang = np.concatenate([row[:, None] * freqs, col[:, None] * freqs], axis=-1).astype(f32)
    cos = np.cos(ang).astype(f32)
    sin = np.sin(ang).astype(f32)
    cosT = np.repeat(cos.T, 2, axis=0)
    sinT = np.repeat(sin.T, 2, axis=0).copy()
    sinT[0::2, :] *= -1.0
    return np.ascontiguousarray(np.concatenate([cosT, sinT], axis=1).astype(f32))


def _tile_w(w, gw):
    n = w.shape[1]
    return np.ascontiguousarray(w.reshape(16, 128, n // gw, gw).transpose(2, 1, 0, 3)).reshape((n // gw) * 128, 16 * gw)


def prepare(inputs, cores):
    f32 = np.float32
    g = {k: np.asarray(v) for k, v in inputs.items()}
    shared = {}
    shared["wmod"] = _tile_w(g["w_mod"][0], 128)
    for f in range(2):
        shared[f"wg{f}"] = _tile_w(g["ffn_wg"][0, f], 256)
        shared[f"wu{f}"] = _tile_w(g["ffn_wu"][0, f], 256)
        wd = g["ffn_wd"][0, f]
        shared[f"wd{f}"] = np.ascontiguousarray(wd.reshape(NG, GF, 128, D).transpose(0, 2, 1, 3)).reshape(NG * 128, GF * D)
    shared["win"] = _tile_w(g["w_in"][0], 256)
    shared["wout"] = _tile_w(g["w_out"][0], 256)
    perm = np.zeros((128, 128), f32)
    for d in range(128):
        perm[d ^ 1, d] = 1.0
    shared["consts"] = np.ascontiguousarray(np.concatenate([np.eye(128, dtype=f32), perm], axis=1))
    shared["gkrow"] = np.ascontiguousarray(np.concatenate([g["q_norm_g"][0], g["k_norm_g"][0]])[None, :].astype(f32))
    bm = g["b_mod"][0]
    shared["rows"] = np.ascontiguousarray(np.stack([g["final_norm_g"], bm[2 * D:3 * D], bm[5 * D:6 * D], bm[8 * D:9 * D]]).astype(f32))

    def colmajor(v):
        v = np.asarray(v, f32).reshape(-1, 16, 128)
        return v.transpose(2, 0, 1)

    in_maps = []
    for c in cores:
        b, half = c // 2, c % 2
        m = dict(shared)
        xs = g["x"][b, half * T:(half + 1) * T]
        cs = g["ctx"][b]
        pos = np.arange(half * T, (half + 1) * T)
        dirs = (0, 1)
        taps = g["conv_w"][0]
        w5 = np.zeros((5, D), f32)
        if half == 0:
            w5[0:4] = taps
        else:
            xs = xs[::-1]
            cs = cs[::-1]
            pos = pos[::-1]
            dirs = (1, 0)
            w5[1] = taps[3]; w5[2] = taps[2]; w5[3] = taps[1]; w5[4] = taps[0]
        m["x"] = np.ascontiguousarray(xs, dtype=f32)
        m["ctx"] = np.ascontiguousarray(cs, dtype=f32)
        m["rope"] = _rope_tables(pos)
        sm = np.zeros((128, NSM), f32)
        cv = np.stack([g["c"][b], g["c_ctx"]])
        sm[:, SM_CV:SM_CV + 32] = colmajor(cv).transpose(0, 2, 1).reshape(128, 32)
        sm[:, SM_NG:SM_NG + 48] = colmajor(g["norm_g"][0]).reshape(128, 48)
        sm[:, SM_BM:SM_BM + 144] = bm.reshape(144, 128).T
        lb = np.stack([g["lru_ba"][0, dirs[0]], g["lru_bx"][0, dirs[0]], g["lru_ba"][0, dirs[1]], g["lru_bx"][0, dirs[1]]])
        sm[:, SM_LB:SM_LB + 64] = colmajor(lb).reshape(128, 64)
        lam = np.stack([g["lru_lambda"][0, dirs[0]], g["lru_lambda"][0, dirs[1]]])
        sm[:, SM_LAM:SM_LAM + 32] = colmajor(lam).reshape(128, 32)
        sm[:, SM_CW:SM_CW + 80] = colmajor(w5).transpose(0, 2, 1).reshape(128, 80)
        sm[:, SM_CB:SM_CB + 16] = colmajor(g["conv_b"][0]).reshape(128, 16)
        sm[:, SM_QG] = g["q_norm_g"][0]
        sm[:, SM_KG] = g["k_norm_g"][0]
        sm[:, SM_FA] = 1.0 if half == 0 else 0.0
        sm[:, SM_FB] = 0.0 if half == 0 else 1.0
        m["smalls"] = sm
        lw = np.stack([g["lru_wa"][0, dirs[0]], g["lru_wx"][0, dirs[0]], g["lru_wa"][0, dirs[1]], g["lru_wx"][0, dirs[1]]])
        m["lruw"] = np.ascontiguousarray(lw.reshape(64, 128, 128).transpose(1, 0, 2)).reshape(128, 64 * 128).astype(f32)
        in_maps.append(m)
    return in_maps


_NC_CACHE = {}


def kernel(**inputs):
    cores = list(range(8))
    in_maps = prepare(inputs, cores)
    if "nc" not in _NC_CACHE:
        _NC_CACHE["nc"] = build(8, False)
    res = run_bass_kernel_spmd(_NC_CACHE["nc"], in_maps, core_ids=cores)
    out = np.empty((4, 4096, D), np.float32)
    for c in cores:
        b, half = c // 2, c % 2
        o = np.asarray(res.results[c]["out"])
        if half == 1:
            o = o[::-1]
        out[b, half * T:(half + 1) * T] = o
    return out
```
